# Optimizing a Trainium2 kernel written in Bass

```python
import math
import jax, jax.numpy as jnp
from jax import lax
import numpy as np

D_MODEL = 4096
BATCH = 4
SEQ = 4096
DEPTH = 2

CHUNK = 64
Q_BLOCK = 128

MIX_WIDTH = D_MODEL
FOX_WIDTH = D_MODEL // 4
SSD_WIDTH = 3 * D_MODEL // 8
GDN_WIDTH = MIX_WIDTH - FOX_WIDTH - SSD_WIDTH

FOX_HEAD_DIM = 128
FOX_HEADS = FOX_WIDTH // FOX_HEAD_DIM

SSD_HEAD_DIM = 64
SSD_HEADS = SSD_WIDTH // SSD_HEAD_DIM
SSD_GROUPS = 4
SSD_HEADS_PER_GROUP = SSD_HEADS // SSD_GROUPS
SSD_STATE = 128
SSD_CONV_DIM = SSD_WIDTH + 2 * SSD_GROUPS * SSD_STATE

GDN_HEAD_DIM = 128
GDN_HEADS = GDN_WIDTH // GDN_HEAD_DIM

CONV_WIDTH = 4
EPS = 1e-6

IN_SIZES = (
    3 * FOX_WIDTH,
    FOX_HEADS,
    FOX_WIDTH,
    SSD_CONV_DIM,
    SSD_WIDTH,
    SSD_HEADS,
    3 * GDN_WIDTH,
    GDN_WIDTH,
    GDN_HEADS,
    GDN_HEADS,
)
IN_WIDTH = sum(IN_SIZES)
IN_SPLIT_POINTS = tuple(int(v) for v in np.cumsum(IN_SIZES)[:-1])

kernel_name = "hybrid_fox_ssd_gdn_parallel_heads"


def rms_norm(x, w):
    xf = x.astype(jnp.float32)
    y = xf * lax.rsqrt(jnp.mean(xf * xf, axis=-1, keepdims=True) + EPS)
    return (y * w.astype(jnp.float32)).astype(x.dtype)


def l2_norm(x):
    xf = x.astype(jnp.float32)
    return (xf * lax.rsqrt(jnp.sum(xf * xf, axis=-1, keepdims=True) + EPS)).astype(x.dtype)


def causal_depthwise_conv(x, w):
    k_width, channels = w.shape
    return lax.conv_general_dilated(
        x, w[:, None, :].astype(x.dtype), window_strides=(1,), padding=[(k_width - 1, 0)],
        dimension_numbers=("NWC", "WIO", "NWC"), feature_group_count=channels)


def fox_attention(q, k, v, log_f):
    bsz, seq, heads, dh = q.shape
    n_blocks = seq // Q_BLOCK
    F = jnp.cumsum(log_f.astype(jnp.float32), axis=1)
    Fk = F.transpose(0, 2, 1)
    kpos = jnp.arange(seq)
    scale = dh ** -0.5
    qb = q.reshape(bsz, n_blocks, Q_BLOCK, heads, dh).transpose(1, 0, 2, 3, 4)
    Fb = F.reshape(bsz, n_blocks, Q_BLOCK, heads).transpose(1, 0, 3, 2)
    qpos = kpos.reshape(n_blocks, Q_BLOCK)

    def block(args):
        qi, Fi, pi = args
        logits = jnp.einsum('bqhd,bkhd->bhqk', qi, k,
                            preferred_element_type=jnp.float32) * scale
        logits = logits + (Fi[..., :, None] - Fk[..., None, :])
        mask = pi[:, None] >= kpos[None, :]
        logits = jnp.where(mask, logits, -jnp.inf)
        p = jax.nn.softmax(logits, axis=-1)
        return jnp.einsum('bhqk,bkhd->bqhd', p.astype(v.dtype), v)

    o = lax.map(block, (qb, Fb, qpos))
    return o.transpose(1, 0, 2, 3, 4).reshape(bsz, seq, heads, dh)


def ssd_scan(x, dt, A, Bm, Cm):
    bsz, seq, heads, hp = x.shape
    groups, n_state = Bm.shape[2], Bm.shape[3]
    r = heads // groups
    nc = seq // CHUNK
    xdt = (x * dt[..., None]).reshape(bsz, nc, CHUNK, groups, r, hp)
    a = (dt * A).reshape(bsz, nc, CHUNK, groups, r)
    Bc = Bm.reshape(bsz, nc, CHUNK, groups, n_state)
    Cc = Cm.reshape(bsz, nc, CHUNK, groups, n_state)
    a_cs = jnp.cumsum(a, axis=2)
    tri = jnp.tril(jnp.ones((CHUNK, CHUNK), dtype=bool))
    seg = a_cs[:, :, :, None] - a_cs[:, :, None, :]
    lmat = jnp.exp(jnp.where(tri[:, :, None, None], seg, -jnp.inf))
    cb = jnp.einsum('bclgn,bcsgn->bclsg', Cc, Bc)
    y_diag = jnp.einsum('bclsgr,bcsgrp->bclgrp', cb[..., None] * lmat, xdt)
    decay_states = jnp.exp(a_cs[:, :, -1:] - a_cs)
    states = jnp.einsum('bclgn,bclgrp->bcgrpn', Bc, xdt * decay_states[..., None])
    chunk_decay = jnp.exp(a_cs[:, :, -1])

    def step(h, inp):
        st, dec = inp
        return h * dec[..., None, None] + st, h

    h0 = jnp.zeros_like(states[:, 0])
    _, h_in = lax.scan(step, h0, (jnp.moveaxis(states, 1, 0), jnp.moveaxis(chunk_decay, 1, 0)))
    h_in = jnp.moveaxis(h_in, 0, 1)
    y_off = jnp.einsum('bclgn,bcgrpn->bclgrp', Cc, h_in) * jnp.exp(a_cs)[..., None]
    return (y_diag + y_off).reshape(bsz, seq, heads, hp)


def gated_delta_rule(q, k, v, g, beta):
    out_dtype = v.dtype
    bsz, seq, heads, dk = q.shape
    dv = v.shape[-1]
    nc = seq // CHUNK

    def chunked(t):
        return t.astype(jnp.float32).reshape(bsz, nc, CHUNK, heads, -1).transpose(0, 3, 1, 2, 4)

    qc = chunked(q) * (dk ** -0.5)
    kc, vc = chunked(k), chunked(v)
    bc = beta.astype(jnp.float32).reshape(bsz, nc, CHUNK, heads).transpose(0, 3, 1, 2)
    gc = jnp.cumsum(g.astype(jnp.float32).reshape(bsz, nc, CHUNK, heads).transpose(0, 3, 1, 2), axis=-1)
    incl = jnp.tril(jnp.ones((CHUNK, CHUNK), dtype=bool))
    strict = jnp.tril(jnp.ones((CHUNK, CHUNK), dtype=bool), k=-1)
    diff = gc[..., :, None] - gc[..., None, :]
    decay = jnp.exp(jnp.where(incl, diff, -jnp.inf))
    k_beta = kc * bc[..., None]
    kk = jnp.einsum('bhcld,bhcsd->bhcls', k_beta, kc) * decay
    a_mat = jnp.where(strict, kk, 0.0) + jnp.eye(CHUNK, dtype=jnp.float32)
    rhs = jnp.concatenate([vc * bc[..., None], k_beta * jnp.exp(gc)[..., None]], axis=-1)
    sol = lax.linalg.triangular_solve(a_mat, rhs, left_side=True, lower=True, unit_diagonal=True)
    u, w = sol[..., :dv], sol[..., dv:]
    qk = jnp.einsum('bhcld,bhcsd->bhcls', qc, kc) * decay
    g_last = gc[..., -1]
    k_tail = kc * jnp.exp(g_last[..., None] - gc)[..., None]
    q_dec = qc * jnp.exp(gc)[..., None]

    def step(state, inp):
        u_i, w_i, qd_i, qk_i, kt_i, gl_i = inp
        v_new = u_i - jnp.einsum('bhlk,bhkv->bhlv', w_i, state)
        o_i = jnp.einsum('bhlk,bhkv->bhlv', qd_i, state) + jnp.einsum('bhls,bhsv->bhlv', qk_i, v_new)
        state = state * jnp.exp(gl_i)[..., None, None] + jnp.einsum('bhlk,bhlv->bhkv', kt_i, v_new)
        return state, o_i

    xs = (jnp.moveaxis(u, 2, 0), jnp.moveaxis(w, 2, 0), jnp.moveaxis(q_dec, 2, 0),
          jnp.moveaxis(qk, 2, 0), jnp.moveaxis(k_tail, 2, 0), jnp.moveaxis(g_last, 2, 0))
    s0 = jnp.zeros((bsz, heads, dk, dv), dtype=jnp.float32)
    _, o = lax.scan(step, s0, xs)
    return o.transpose(1, 0, 3, 2, 4).reshape(bsz, seq, heads, dv).astype(out_dtype)


def hybrid_layer(x, norm_w, w_in, w_out, fox_b_f, fox_q_norm_w, fox_k_norm_w, fox_out_norm_w,
                 ssd_conv_w, ssd_conv_b, ssd_dt_bias, ssd_A_log, ssd_D, ssd_norm_w,
                 gdn_conv_w, gdn_dt_bias, gdn_A_log, gdn_norm_w):
    bsz, seq, _ = x.shape
    h = rms_norm(x, norm_w)
    proj = jnp.einsum('bsd,de->bse', h, w_in)
    (fox_qkv, fox_f, fox_z, ssd_xbc, ssd_z, ssd_dt,
     gdn_qkv, gdn_z, gdn_beta, gdn_a) = jnp.split(proj, IN_SPLIT_POINTS, axis=-1)

    q, k, v = jnp.split(fox_qkv, 3, axis=-1)
    q = rms_norm(q.reshape(bsz, seq, FOX_HEADS, FOX_HEAD_DIM), fox_q_norm_w)
    k = rms_norm(k.reshape(bsz, seq, FOX_HEADS, FOX_HEAD_DIM), fox_k_norm_w)
    v = v.reshape(bsz, seq, FOX_HEADS, FOX_HEAD_DIM)
    log_f = jax.nn.log_sigmoid((fox_f + fox_b_f).astype(jnp.float32))
    o_fox = fox_attention(q, k, v, log_f)
    o_fox = rms_norm(o_fox, fox_out_norm_w).reshape(bsz, seq, FOX_WIDTH) * jax.nn.silu(fox_z)

    xbc = jax.nn.silu(causal_depthwise_conv(ssd_xbc, ssd_conv_w) + ssd_conv_b)
    xs, Bm, Cm = jnp.split(xbc, (SSD_WIDTH, SSD_WIDTH + SSD_GROUPS * SSD_STATE), axis=-1)
    xs = xs.reshape(bsz, seq, SSD_HEADS, SSD_HEAD_DIM)
    dt = jax.nn.softplus((ssd_dt + ssd_dt_bias).astype(jnp.float32))
    A = -jnp.exp(ssd_A_log.astype(jnp.float32))
    y = ssd_scan(xs, dt, A, Bm.reshape(bsz, seq, SSD_GROUPS, SSD_STATE),
                 Cm.reshape(bsz, seq, SSD_GROUPS, SSD_STATE))
    y = y + xs * ssd_D[:, None]
    y = y.reshape(bsz, seq, SSD_WIDTH) * jax.nn.silu(ssd_z)
    group_w = SSD_HEADS_PER_GROUP * SSD_HEAD_DIM
    o_ssd = rms_norm(y.reshape(bsz, seq, SSD_GROUPS, group_w),
                     ssd_norm_w.reshape(SSD_GROUPS, group_w)).reshape(bsz, seq, SSD_WIDTH)

    qkv = jax.nn.silu(causal_depthwise_conv(gdn_qkv, gdn_conv_w))
    gq, gk, gv = jnp.split(qkv, 3, axis=-1)
    gq = l2_norm(gq.reshape(bsz, seq, GDN_HEADS, GDN_HEAD_DIM))
    gk = l2_norm(gk.reshape(bsz, seq, GDN_HEADS, GDN_HEAD_DIM))
    gv = gv.reshape(bsz, seq, GDN_HEADS, GDN_HEAD_DIM)
    beta = jax.nn.sigmoid(gdn_beta.astype(jnp.float32))
    g = -jnp.exp(gdn_A_log.astype(jnp.float32)) * jax.nn.softplus((gdn_a + gdn_dt_bias).astype(jnp.float32))
    o_gdn = gated_delta_rule(gq, gk, gv, g, beta)
    o_gdn = rms_norm(o_gdn, gdn_norm_w).reshape(bsz, seq, GDN_WIDTH) * jax.nn.silu(gdn_z)

    mix = jnp.concatenate([o_fox.astype(x.dtype), o_ssd.astype(x.dtype), o_gdn.astype(x.dtype)], axis=-1)
    return x + jnp.einsum('bse,ed->bsd', mix, w_out)


def _inv_softplus_dt(key, shape):
    dt = jnp.exp(jax.random.uniform(key, shape, minval=math.log(1e-3), maxval=math.log(1e-1)))
    return dt + jnp.log(-jnp.expm1(-dt))


def setup_inputs(seed: int = 0) -> dict:
    key = jax.random.key(seed)
    ks = jax.random.split(key, 18)
    f32 = jnp.float32

    def gain(k, shape):
        return 1.0 + 0.02 * jax.random.normal(k, shape, f32)

    return {
        "x": jax.random.normal(ks[0], (BATCH, SEQ, D_MODEL), f32),
        "norm_w": gain(ks[1], (DEPTH, D_MODEL)),
        "w_in": jax.random.normal(ks[2], (DEPTH, D_MODEL, IN_WIDTH), f32) * D_MODEL ** -0.5,
        "w_out": jax.random.normal(ks[3], (DEPTH, MIX_WIDTH, D_MODEL), f32) * (0.5 * MIX_WIDTH ** -0.5),
        "fox_b_f": jax.random.uniform(ks[4], (DEPTH, FOX_HEADS), f32, minval=1.0, maxval=4.0),
        "fox_q_norm_w": gain(ks[5], (DEPTH, FOX_HEAD_DIM)),
        "fox_k_norm_w": gain(ks[6], (DEPTH, FOX_HEAD_DIM)),
        "fox_out_norm_w": gain(ks[7], (DEPTH, FOX_HEAD_DIM)),
        "ssd_conv_w": jax.random.normal(ks[8], (DEPTH, CONV_WIDTH, SSD_CONV_DIM), f32) * CONV_WIDTH ** -0.5,
        "ssd_conv_b": 0.02 * jax.random.normal(ks[9], (DEPTH, SSD_CONV_DIM), f32),
        "ssd_dt_bias": _inv_softplus_dt(ks[10], (DEPTH, SSD_HEADS)),
        "ssd_A_log": jnp.log(jax.random.uniform(ks[11], (DEPTH, SSD_HEADS), f32, minval=1.0, maxval=16.0)),
        "ssd_D": gain(ks[12], (DEPTH, SSD_HEADS)),
        "ssd_norm_w": gain(ks[13], (DEPTH, SSD_WIDTH)),
        "gdn_conv_w": jax.random.normal(ks[14], (DEPTH, CONV_WIDTH, 3 * GDN_WIDTH), f32) * CONV_WIDTH ** -0.5,
        "gdn_dt_bias": _inv_softplus_dt(ks[15], (DEPTH, GDN_HEADS)),
        "gdn_A_log": jnp.log(jax.random.uniform(ks[16], (DEPTH, GDN_HEADS), f32, minval=1.0, maxval=16.0)),
        "gdn_norm_w": gain(ks[17], (DEPTH, GDN_HEAD_DIM)),
    }


def reference(x, norm_w, w_in, w_out, fox_b_f, fox_q_norm_w, fox_k_norm_w, fox_out_norm_w,
              ssd_conv_w, ssd_conv_b, ssd_dt_bias, ssd_A_log, ssd_D, ssd_norm_w,
              gdn_conv_w, gdn_dt_bias, gdn_A_log, gdn_norm_w):
    for l in range(DEPTH):
        x = hybrid_layer(x, norm_w[l], w_in[l], w_out[l], fox_b_f[l], fox_q_norm_w[l],
                         fox_k_norm_w[l], fox_out_norm_w[l], ssd_conv_w[l], ssd_conv_b[l],
                         ssd_dt_bias[l], ssd_A_log[l], ssd_D[l], ssd_norm_w[l],
                         gdn_conv_w[l], gdn_dt_bias[l], gdn_A_log[l], gdn_norm_w[l])
    return x
```

```python
import numpy as np
import concourse.bass as bass
import concourse.mybir as mybir
from concourse.bass_utils import run_bass_kernel_spmd
from contextlib import ExitStack

F32 = mybir.dt.float32
BF16 = mybir.dt.bfloat16
F32R = mybir.dt.float32r
AF = mybir.ActivationFunctionType
ALU = mybir.AluOpType
AX = mybir.AxisListType


class Buf:
    def __init__(self, t, name, share=None):
        self.t = t
        self.name = name
        self.s = share.s if share is not None else {"lw": None, "rd": {}}
        self.dkey = None

    @property
    def lw(self):
        return self.s["lw"]

    @lw.setter
    def lw(self, v):
        self.s["lw"] = v

    @property
    def rd(self):
        return self.s["rd"]

    @rd.setter
    def rd(self, v):
        self.s["rd"] = v

    def __getitem__(self, idx):
        return self.t[idx]


class KB:
    def __init__(self, nc):
        self.nc = nc
        self.eng = {"pe": nc.tensor, "act": nc.scalar, "dve": nc.vector,
                    "pool": nc.gpsimd, "sp": nc.sync}
        self.sems = {}
        self.cnt = {}
        for e in ("pe", "act", "dve", "pool"):
            self.sems[e] = nc.alloc_semaphore("s_" + e)
            self.cnt[e] = 0
        self.waited = {}
        self.nid = 0
        self.dbufs = []
        self.free_dkeys = []
        self.dcnt = {}
        self.nwaits = 0
        self.nops = 0

    def uid(self, p):
        self.nid += 1
        return "%s_%d" % (p, self.nid)

    def sb(self, stack, shape, dt, name="sb"):
        t = stack.enter_context(self.nc.sbuf_tensor(self.uid(name), list(shape), dt))
        return Buf(t, name)

    def ps(self, stack, shape, dt, name="ps"):
        t = stack.enter_context(self.nc.psum_tensor(self.uid(name), list(shape), dt))
        return Buf(t, name)

    def dram(self, name, shape, dt, kind=None):
        if kind is None:
            t = self.nc.dram_tensor(name, list(shape), dt)
        else:
            t = self.nc.dram_tensor(name, list(shape), dt, kind=kind)
        return Buf(t.ap(), name)

    def view(self, ap, name="v"):
        return Buf(ap, name)

    def _wait(self, e, key, val):
        k = (e, key)
        if self.waited.get(k, 0) >= val:
            return
        if key == e:
            assert val <= self.cnt[e], "self-wait on pending stamp (%s)" % e
        self.eng[e].wait_ge(self.sems[key], val)
        self.waited[k] = val
        self.nwaits += 1

    def op(self, e, fn, reads=(), writes=(), inc=True):
        for b in reads:
            if b.lw is not None:
                self._wait(e, *b.lw)
        for b in writes:
            if b.lw is not None and (b.lw[0] != e or (e != "pe" and b.lw[1] <= self.cnt[e])):
                self._wait(e, *b.lw)
            for key, val in b.rd.items():
                if key != e or (e != "pe" and val <= self.cnt[e]):
                    self._wait(e, key, val)
        ins = fn(self.eng[e])
        self.nops += 1
        if inc:
            self.cnt[e] += 1
            ins.then_inc(self.sems[e], 1)
            v = self.cnt[e]
        else:
            v = self.cnt[e] + 1
        for b in reads:
            if b.rd.get(e, 0) < v:
                b.rd[e] = v
        for b in writes:
            b.lw = (e, v)
            b.rd = {}
        return ins

    def dma(self, q, out, in_, wbufs, rbufs, owner):
        b = owner
        if b.dkey is None:
            if self.free_dkeys:
                b.dkey = self.free_dkeys.pop()
            else:
                b.dkey = self.uid("d")
                self.sems[b.dkey] = self.nc.alloc_semaphore("s_" + b.dkey)
                self.dcnt[b.dkey] = 0
            self.dbufs.append(b)
        for rb in rbufs:
            if rb.lw is not None:
                self._wait(q, *rb.lw)
        for wb in wbufs:
            if wb.lw is not None and wb.lw[0] != b.dkey:
                self._wait(q, *wb.lw)
            for key, val in wb.rd.items():
                self._wait(q, key, val)
        ins = self.eng[q].dma_start(out=out, in_=in_)
        self.nops += 1
        self.dcnt[b.dkey] += 16
        c = self.dcnt[b.dkey]
        ins.then_inc(self.sems[b.dkey], 16)
        for rb in rbufs:
            rb.rd[b.dkey] = c
        for wb in wbufs:
            wb.lw = (b.dkey, c)
            wb.rd = {}
        return ins

    def collective(self, kind, groups, in_ap, out_ap, in_buf, out_buf):
        if "cc" not in self.sems:
            self.sems["cc"] = self.nc.alloc_semaphore("s_cc")
            self.dcnt["cc"] = 0
        ins = self.eng["pool"].collective_compute(kind, ALU.bypass, replica_groups=groups,
                                                  ins=[in_ap], outs=[out_ap])
        self.dcnt["cc"] += 1
        ins.then_inc(self.sems["cc"], 1)
        in_buf.rd["cc"] = self.dcnt["cc"]
        out_buf.lw = ("cc", self.dcnt["cc"])
        out_buf.rd = {}

    def barrier(self):
        for e in ("pe", "act", "dve", "pool", "sp"):
            for key in ("pe", "act", "dve", "pool"):
                if key != e and self.cnt[key] > 0:
                    self._wait(e, key, self.cnt[key])
            for key, c in self.dcnt.items():
                if c > 0:
                    self._wait(e, key, c)

    def release_dsems(self):
        for b in self.dbufs:
            if b.dkey is not None:
                self.free_dkeys.append(b.dkey)
                b.dkey = None
        self.dbufs = []


D = 4096
NKC = D // 128
EPS = 1e-6
NEC = 57
EC_FQ, EC_FK, EC_FV, EC_FZ = 0, 4, 8, 12
EC_SX, EC_SB, EC_SC, EC_SZ = 16, 22, 24, 26
EC_GQ, EC_GK, EC_GV, EC_GZ = 32, 38, 44, 50
EC_SM = 56

_OFF = np.cumsum([0, 3072, 8, 1024, 2560, 1536, 24, 4608, 1536, 12, 12])


def local_cols(hh):
    o = _OFF
    c = []
    r = np.arange
    c += list(o[0] + hh * 512 + r(512))
    c += list(o[0] + 1024 + hh * 512 + r(512))
    c += list(o[0] + 2048 + hh * 512 + r(512))
    c += list(o[2] + hh * 512 + r(512))
    c += list(o[3] + hh * 768 + r(768))
    c += list(o[3] + 1536 + hh * 256 + r(256))
    c += list(o[3] + 2048 + hh * 256 + r(256))
    c += list(o[4] + hh * 768 + r(768))
    c += list(o[6] + hh * 768 + r(768))
    c += list(o[6] + 1536 + hh * 768 + r(768))
    c += list(o[6] + 3072 + hh * 768 + r(768))
    c += list(o[7] + hh * 768 + r(768))
    c += list(o[1] + hh * 4 + r(4))
    c += list(o[5] + hh * 12 + r(12))
    c += list(o[8] + hh * 6 + r(6))
    c += list(o[9] + hh * 6 + r(6))
    c += [-1] * (NEC * 128 - len(c))
    return np.array(c, dtype=np.int64)


def prep_win(w_in_l, hh):
    cols = local_cols(hh)
    wz = np.concatenate([w_in_l, np.zeros((D, 1), np.float32)], axis=1)
    wl = wz[:, cols]
    wl = wl.reshape(NKC, 128, NEC, 128).transpose(2, 1, 0, 3)
    return np.ascontiguousarray(wl).reshape(NEC, 128, NKC * 128)


def stage_inproj(kb, S, x_d, win_d, nwb_d, proj_d, C, TT=512, ec_list=None, xload=None):
    if ec_list is None:
        ec_list = list(range(NEC))
    nT = S // TT
    nsub = TT // 128
    with ExitStack() as st:
        xs = [kb.sb(st, [128, D], F32, "xs") for _ in range(2)]
        junk = kb.sb(st, [128, D], BF16, "junk")
        hb = [kb.sb(st, [128, D], BF16, "hb") for _ in range(2)]
        nwb = kb.sb(st, [128, D], F32, "nwb")
        hT = [kb.sb(st, [128, NKC, TT], BF16, "hT") for _ in range(2)]
        wb = [kb.sb(st, [128, NKC, 128], BF16, "wb") for _ in range(3)]
        stg = [kb.sb(st, [128, TT], F32, "stg") for _ in range(3)]
        ss = [kb.sb(st, [128, 1], F32, "ss") for _ in range(2)]
        rs = [kb.sb(st, [128, 1], F32, "rs") for _ in range(2)]
        pst = [kb.ps(st, [128, 512], BF16, "pst") for _ in range(2)]
        psm = [kb.ps(st, [128, 512], F32, "psm") for _ in range(3)]
        ident = C["ident_bf"]
        epsb = C["eps"]

        kb.dma("sp", nwb[:], nwb_d, [nwb], [], nwb)
        nsubs = S // 128

        def load_x(j):
            if xload is not None:
                xload(kb, xs[j % 2], j)
            else:
                kb.dma("sp", xs[j % 2][:], x_d[j * 128:(j + 1) * 128, :], [xs[j % 2]], [], xs[j % 2])

        cnt = {"ev": 0, "pt": 0}

        def norm_tile(tt):
            for js in range(nsub):
                j = tt * nsub + js
                if j + 1 < nsubs:
                    load_x(j + 1)
                x_, h_, s_, r_ = xs[j % 2], hb[j % 2], ss[j % 2], rs[j % 2]
                kb.op("act", lambda e: e.activation(out=junk[:], in_=x_[:], func=AF.Square,
                                                    accum_out=s_[:]),
                      reads=[x_], writes=[junk, s_])
                kb.op("act", lambda e: e.activation(out=r_[:], in_=s_[:], func=AF.Ln,
                                                    scale=1.0 / D, bias=epsb[:]),
                      reads=[s_, epsb], writes=[r_])
                kb.op("act", lambda e: e.activation(out=r_[:], in_=r_[:], func=AF.Exp, scale=-0.5), reads=[r_], writes=[r_])
                kb.op("dve", lambda e: e.scalar_tensor_tensor(out=h_[:], in0=x_[:], scalar=r_[:, 0:1],
                                                              in1=nwb[:], op0=ALU.mult, op1=ALU.mult),
                      reads=[x_, r_, nwb], writes=[h_])
                for g in range(NKC // 4):
                    p_ = pst[cnt["pt"] % 2]
                    cnt["pt"] += 1
                    for q in range(4):
                        kc = g * 4 + q
                        kb.op("pe", lambda e: e.transpose(out=p_[:, q * 128:(q + 1) * 128],
                                                          in_=h_[:, kc * 128:(kc + 1) * 128],
                                                          identity=ident[:]),
                              reads=[h_, ident], writes=[p_], inc=(q == 3))
                    ev = "dve" if cnt["ev"] % 2 == 0 else "act"
                    cnt["ev"] += 1
                    dst = hT[tt % 2][:, g * 4:(g + 1) * 4, js * 128:(js + 1) * 128]
                    src = p_[:].rearrange("p (a b) -> p a b", a=4)
                    if ev == "dve":
                        kb.op("dve", lambda e: e.tensor_copy(out=dst, in_=src), reads=[p_], writes=[hT[tt % 2]])
                    else:
                        kb.op("act", lambda e: e.activation(out=dst, in_=src, func=AF.Copy),
                              reads=[p_], writes=[hT[tt % 2]])

        wq = {"n": 0}
        seq = [(tt, ec) for tt in range(nT) for ec in ec_list]

        def load_w(i):
            if i < len(seq):
                ec = seq[i][1]
                b_ = wb[i % 3]
                kb.dma("pool", b_[:].rearrange("p a b -> p (a b)"), win_d[ec], [b_], [], b_)

        load_x(0)
        norm_tile(0)
        load_w(0)
        load_w(1)
        i = 0
        for tt in range(nT):
            if tt + 1 < nT:
                norm_tile(tt + 1)
            for ec in ec_list:
                load_w(i + 2)
                w_ = wb[i % 3]
                p_ = psm[i % 3]
                s_ = stg[i % 3]
                h_ = hT[tt % 2]
                for kc in range(NKC):
                    kb.op("pe", lambda e: e.matmul(p_[:, :TT], lhsT=w_[:, kc, :], rhs=h_[:, kc, :],
                                                   start=(kc == 0), stop=(kc == NKC - 1)),
                          reads=[w_, h_], writes=[p_], inc=(kc == NKC - 1))
                if i % 2 == 0:
                    kb.op("act", lambda e: e.activation(out=s_[:], in_=p_[:, :TT], func=AF.Copy),
                          reads=[p_], writes=[s_])
                else:
                    kb.op("dve", lambda e: e.tensor_copy(out=s_[:], in_=p_[:, :TT]), reads=[p_], writes=[s_])
                kb.dma("sp", proj_d[ec * 128:(ec + 1) * 128, tt * TT:(tt + 1) * TT], s_[:],
                       [Buf(None, "projreg")], [s_], s_)
                i += 1
        kb.barrier()
        kb.release_dsems()


NEGM = -30000.0
MIXDT = BF16
SMW = 32


def const_arrays():
    c = {}
    r = np.arange(128)
    c["ident_bf"] = np.eye(128, dtype=np.float32)
    c["ident_f"] = np.eye(128, dtype=np.float32)
    c["ones_f"] = np.ones((128, 128), np.float32)
    c["eps"] = np.full((128, 1), EPS, np.float32)
    c["tri_f"] = (r[:, None] <= r[None, :]).astype(np.float32)
    c["tri_bf"] = c["tri_f"].copy()
    c["striT"] = (r[:, None] > r[None, :]).astype(np.float32)
    c["negm_ns"] = np.where(r[:, None] > r[None, :], NEGM, 0.0).astype(np.float32)
    c["negm_s"] = np.where(r[:, None] >= r[None, :], NEGM, 0.0).astype(np.float32)
    c["negm_sT"] = np.where(r[None, :] >= r[:, None], NEGM, 0.0).astype(np.float32)
    c["negm_ns4"] = np.tile(c["negm_ns"], (1, 4))
    c["half64"] = np.broadcast_to((r[:, None] <= 63), (128, 128)).astype(np.float32).copy()
    return c


CONST_DT = {"ident_bf": BF16, "tri_bf": BF16}


def load_consts(kb, st, cd):
    C = {}
    for name, (ap, shape, dt) in cd.items():
        b = kb.sb(st, shape, dt, name)
        q = "pool" if dt == BF16 else "sp"
        kb.dma(q, b[:], ap, [b], [], b)
        C[name] = b
    return C


PC = {"fwq": 0, "fwk": 1, "fwo": 2, "gnw": 3, "snw": 4, "sconvb": 10, "sconv": 20, "gconv": 60}
NPC = 60 + 18 * 4


def prep_params(inp, l, hh):
    o = _OFF
    cols = local_cols(hh)
    pc = np.zeros((128, NPC), np.float32)
    pc[:, PC["fwq"]] = inp["fox_q_norm_w"][l]
    pc[:, PC["fwk"]] = inp["fox_k_norm_w"][l]
    pc[:, PC["fwo"]] = inp["fox_out_norm_w"][l]
    pc[:, PC["gnw"]] = inp["gdn_norm_w"][l]
    for i in range(6):
        pc[:, PC["snw"] + i] = inp["ssd_norm_w"][l][hh * 768 + i * 128: hh * 768 + (i + 1) * 128]
    for i in range(10):
        ch = cols[(EC_SX + i) * 128:(EC_SX + i + 1) * 128] - o[3]
        pc[:, PC["sconvb"] + i] = inp["ssd_conv_b"][l][ch]
        pc[:, PC["sconv"] + 4 * i: PC["sconv"] + 4 * i + 4] = inp["ssd_conv_w"][l][:, ch].T
    for i in range(18):
        ch = cols[(EC_GQ + i) * 128:(EC_GQ + i + 1) * 128] - o[6]
        pc[:, PC["gconv"] + 4 * i: PC["gconv"] + 4 * i + 4] = inp["gdn_conv_w"][l][:, ch].T
    smb = np.zeros(SMW, np.float32)
    sgn = np.ones(SMW, np.float32)
    alog = np.zeros(SMW, np.float32)
    smb[0:4] = inp["fox_b_f"][l][hh * 4: hh * 4 + 4]
    smb[4:16] = inp["ssd_dt_bias"][l][hh * 12: hh * 12 + 12]
    smb[22:28] = inp["gdn_dt_bias"][l][hh * 6: hh * 6 + 6]
    sgn[0:4] = -1.0
    sgn[16:22] = -1.0
    alog[4:16] = inp["ssd_A_log"][l][hh * 12: hh * 12 + 12]
    alog[22:28] = inp["gdn_A_log"][l][hh * 6: hh * 6 + 6]
    dsk = np.repeat(inp["ssd_D"][l][hh * 12: hh * 12 + 12], 64)
    pr = np.concatenate([smb, sgn, alog, dsk]).astype(np.float32)
    pr = np.broadcast_to(pr, (128, pr.size)).copy()
    return pc, pr


NPR = 3 * SMW + 768


def run_interleaved(gens):
    gens = list(gens)
    while gens:
        for g in list(gens):
            try:
                next(g)
            except StopIteration:
                gens.remove(g)


def pnorm(kb, W, src, dst, S, scale, wcol, mul):
    C = W["C"]
    for b0 in range(0, S, 512):
        n = min(512, S - b0)
        i = W["pn_i"]
        W["pn_i"] += 1
        sq, rt, pp = W["sq"][i % 2], W["rt"][i % 2], W["pn"][i % 2]
        kb.op("act", lambda e: e.activation(out=sq[:, :n], in_=src[:, b0:b0 + n], func=AF.Square),
              reads=[src], writes=[sq])
        kb.op("pe", lambda e: e.matmul(pp[:, :n], lhsT=C["ones_f"][:], rhs=sq[:, :n], start=True, stop=True),
              reads=[C["ones_f"], sq], writes=[pp])
        kb.op("act", lambda e: e.activation(out=rt[:, :n], in_=pp[:, :n], func=AF.Ln, scale=scale,
                                            bias=C["eps"][:]), reads=[pp, C["eps"]], writes=[rt])
        kb.op("act", lambda e: e.activation(out=rt[:, :n], in_=rt[:, :n], func=AF.Exp, scale=-0.5), reads=[rt], writes=[rt])
        if wcol is None:
            kb.op("dve", lambda e: e.scalar_tensor_tensor(out=dst[:, b0:b0 + n], in0=src[:, b0:b0 + n], scalar=mul,
                                                          in1=rt[:, :n], op0=ALU.mult, op1=ALU.mult),
                  reads=[src, rt], writes=[dst])
        else:
            kb.op("dve", lambda e: e.scalar_tensor_tensor(out=dst[:, b0:b0 + n], in0=src[:, b0:b0 + n], scalar=wcol,
                                                          in1=rt[:, :n], op0=ALU.mult, op1=ALU.mult),
                  reads=[src, rt, W["pc"]], writes=[dst])


def conv_silu(kb, W, proj_d, ec, wofs, bofs, dst, S, fp32dst):
    raw = W["craw"]
    acc = dst if fp32dst else W["cacc"]
    pc = W["pc"]
    kb.dma("sp", raw[:, 3:3 + S], proj_d[ec * 128:(ec + 1) * 128, 0:S], [raw], [], raw)
    kb.op("dve", lambda e: e.tensor_scalar(out=acc[:, :S], in0=raw[:, 0:S], scalar1=pc[:, wofs:wofs + 1], scalar2=None,
                                           op0=ALU.mult), reads=[raw, pc], writes=[acc])
    for k in range(1, 4):
        kb.op("dve", lambda e: e.scalar_tensor_tensor(out=acc[:, :S], in0=raw[:, k:k + S],
                                                      scalar=pc[:, wofs + k:wofs + k + 1], in1=acc[:, :S],
                                                      op0=ALU.mult, op1=ALU.add), reads=[raw, pc, acc], writes=[acc])
    if bofs is None:
        kb.op("act", lambda e: e.activation(out=dst[:, :S], in_=acc[:, :S], func=AF.Silu), reads=[acc], writes=[dst])
    else:
        kb.op("act", lambda e: e.activation(out=dst[:, :S], in_=acc[:, :S], func=AF.Silu, bias=pc[:, bofs:bofs + 1]),
              reads=[acc, pc], writes=[dst])


def alloc_conv(kb, st, W, S):
    W["craw"] = kb.sb(st, [128, S + 3], F32, "craw")
    W["cacc"] = kb.sb(st, [128, S], F32, "cacc")
    kb.op("dve", lambda e: e.memset(W["craw"][:, 0:3], 0.0), reads=[], writes=[W["craw"]])


def alloc_common(kb, st, S, C, pc_d, pr_d):
    W = {"C": C, "pn_i": 0, "cv_i": 0}
    W["pc"] = kb.sb(st, [128, NPC], F32, "pc")
    W["pr"] = kb.sb(st, [128, NPR], F32, "pr")
    kb.dma("sp", W["pc"][:], pc_d, [W["pc"]], [], W["pc"])
    kb.dma("sp", W["pr"][:], pr_d, [W["pr"]], [], W["pr"])
    W["sq"] = [kb.sb(st, [128, 512], F32, "sq") for _ in range(2)]
    W["rt"] = [kb.sb(st, [128, 512], F32, "rt") for _ in range(2)]
    return W


def stage_small(kb, st, W, S, proj_d):
    C = W["C"]
    nblk = S // 128
    T = {}
    T["rate"] = kb.sb(st, [128, nblk, SMW], F32, "rate")
    T["sp"] = kb.sb(st, [128, nblk, SMW], F32, "sp")
    T["beta"] = kb.sb(st, [128, nblk, SMW], F32, "beta")
    with ExitStack() as s2:
        smr = kb.sb(s2, [SMW, S], F32, "smr")
        sm = kb.sb(s2, [128, nblk, SMW], F32, "sm")
        ax = kb.sb(s2, [128, nblk, SMW], F32, "ax")
        nega = kb.sb(s2, [128, SMW], F32, "nega")
        pp = W["pn"][0]
        kb.dma("sp", smr[:], proj_d[EC_SM * 128:EC_SM * 128 + SMW, 0:S], [smr], [], smr)
        for b0 in range(0, nblk, 16):
            nb = min(16, nblk - b0)
            for j in range(nb):
                blk = b0 + j
                kb.op("pe", lambda e: e.transpose(out=pp[:, j * SMW:(j + 1) * SMW],
                                                  in_=smr[0:SMW, blk * 128:(blk + 1) * 128],
                                                  identity=C["ident_f"][0:SMW, 0:SMW]),
                      reads=[smr, C["ident_f"]], writes=[pp], inc=(j == nb - 1))
            kb.op("dve", lambda e: e.tensor_copy(out=sm[:, b0:b0 + nb, :].rearrange("p a b -> p (a b)"),
                                                 in_=pp[:, :nb * SMW]), reads=[pp], writes=[sm])
        pr = W["pr"]

        def bc(off):
            return pr[:, off:off + SMW].unsqueeze(1).broadcast_to([128, nblk, SMW])
        kb.op("dve", lambda e: e.tensor_tensor(out=sm[:], in0=sm[:], in1=bc(0), op=ALU.add), reads=[sm, pr], writes=[sm])
        kb.op("dve", lambda e: e.tensor_tensor(out=sm[:], in0=sm[:], in1=bc(SMW), op=ALU.mult), reads=[sm, pr], writes=[sm])
        kb.op("act", lambda e: e.activation(out=ax[:], in_=sm[:], func=AF.Abs), reads=[sm], writes=[ax])
        kb.op("act", lambda e: e.activation(out=ax[:], in_=ax[:], func=AF.Exp, scale=-1.0), reads=[ax], writes=[ax])
        kb.op("act", lambda e: e.activation(out=ax[:], in_=ax[:], func=AF.Ln, bias=1.0), reads=[ax], writes=[ax])
        kb.op("dve", lambda e: e.scalar_tensor_tensor(out=T["sp"][:], in0=sm[:], scalar=0.0, in1=ax[:],
                                                      op0=ALU.max, op1=ALU.add), reads=[sm, ax], writes=[T["sp"]])
        kb.op("act", lambda e: e.activation(out=nega[:], in_=pr[:, 2 * SMW:3 * SMW], func=AF.Exp), reads=[pr], writes=[nega])
        kb.op("dve", lambda e: e.scalar_tensor_tensor(out=T["rate"][:], in0=T["sp"][:], scalar=-1.0,
                                                      in1=nega[:].unsqueeze(1).broadcast_to([128, nblk, SMW]),
                                                      op0=ALU.mult, op1=ALU.mult), reads=[T["sp"], nega], writes=[T["rate"]])
        kb.op("act", lambda e: e.activation(out=T["beta"][:], in_=T["rate"][:], func=AF.Exp), reads=[T["rate"]],
              writes=[T["beta"]])
        kb.barrier()
    return T


def stage_fox(kb, W, T, S, proj_d, mix_d):
    C = W["C"]
    pc = W["pc"]
    nblk = S // 128
    with ExitStack() as st:
        raw = [kb.sb(st, [128, S], F32, "fraw") for _ in range(2)]
        Fcol = kb.sb(st, [128, nblk, 4], F32, "Fcol")
        pre = kb.sb(st, [128, nblk + 1, 4], F32, "pre")
        bsum = kb.sb(st, [128, nblk, 4], F32, "bsum")
        Fmid = kb.sb(st, [128, nblk, 4], F32, "Fmid")
        lfc = kb.sb(st, [128, nblk, 4], F32, "lfc")
        ptr = W["pn"][1]
        slots = []
        for sl in range(2):
            X = {}
            X["qn"] = kb.sb(st, [128, S], BF16, "qn")
            X["kn"] = kb.sb(st, [128, S], BF16, "kn")
            X["vtok"] = kb.sb(st, [128, nblk, 132], BF16, "vtok")
            X["sz"] = kb.sb(st, [128, S], F32, "sz")
            X["mixo"] = kb.sb(st, [128, S], MIXDT, "mixo")
            X["pt"] = [kb.sb(st, [128, 4, 128], BF16, "pt") for _ in range(3)]
            X["bias"] = [kb.sb(st, [128, nblk], F32, "fbias") for _ in range(2)]
            X["osb"] = [kb.sb(st, [128, 128], F32, "osb") for _ in range(2)]
            X["col"] = [kb.sb(st, [128, 4], F32, "fcol") for _ in range(2)]
            X["junk"] = kb.sb(st, [128, 128], F32, "fjunk")
            X["pss"] = [kb.ps(st, [128, 512], F32, "pss") for _ in range(2)]
            X["po"] = kb.ps(st, [128, 512], F32, "po")
            kb.op("dve", lambda e: e.memset(X["vtok"][:, :, 128:129], 1.0), reads=[], writes=[X["vtok"]])
            slots.append(X)
        logf = T["rate"]

        lf = logf[:, :, 0:4]
        pp = W["pn"][0]
        pv = pp[:, 0:nblk * 4].rearrange("p (a b) -> p a b", b=4)
        kb.op("dve", lambda e: e.tensor_copy(out=lfc[:], in_=lf), reads=[logf], writes=[lfc])
        lff = lfc[:].rearrange("p a b -> p (a b)")
        kb.op("pe", lambda e: e.matmul(pp[:, 0:nblk * 4], lhsT=C["tri_f"][:], rhs=lff, start=True, stop=True),
              reads=[C["tri_f"], lfc], writes=[pp])
        kb.op("dve", lambda e: e.tensor_copy(out=Fcol[:], in_=pv), reads=[pp], writes=[Fcol])
        kb.op("pe", lambda e: e.matmul(pp[:, 0:nblk * 4], lhsT=C["ones_f"][:], rhs=lff, start=True, stop=True),
              reads=[C["ones_f"], lfc], writes=[pp])
        kb.op("dve", lambda e: e.tensor_copy(out=bsum[:], in_=pv), reads=[pp], writes=[bsum])
        kb.op("pe", lambda e: e.matmul(pp[:, 0:nblk * 4], lhsT=C["half64"][:], rhs=lff, start=True, stop=True),
              reads=[C["half64"], lfc], writes=[pp])
        kb.op("dve", lambda e: e.tensor_copy(out=Fmid[:], in_=pv), reads=[pp], writes=[Fmid])
        kb.op("dve", lambda e: e.memset(pre[:, 0, :], 0.0), reads=[], writes=[pre])
        for j in range(nblk):
            kb.op("dve", lambda e: e.tensor_tensor(out=pre[:, j + 1, :], in0=pre[:, j, :], in1=bsum[:, j, :], op=ALU.add),
                  reads=[pre, bsum], writes=[pre])
        kb.op("dve", lambda e: e.tensor_tensor(out=Fcol[:], in0=Fcol[:], in1=pre[:, 0:nblk, :], op=ALU.add),
              reads=[Fcol, pre], writes=[Fcol])
        kb.op("dve", lambda e: e.tensor_tensor(out=Fmid[:], in0=Fmid[:], in1=pre[:, 0:nblk, :], op=ALU.add),
              reads=[Fmid, pre], writes=[Fmid])

        sc = 128.0 ** -0.5

        def setup(X, h):
            qn, kn, vtok, sz = X["qn"], X["kn"], X["vtok"], X["sz"]
            kb.dma("sp", raw[0][:], proj_d[(EC_FQ + h) * 128:(EC_FQ + h + 1) * 128, 0:S], [raw[0]], [], raw[0])
            kb.dma("sp", raw[1][:], proj_d[(EC_FK + h) * 128:(EC_FK + h + 1) * 128, 0:S], [raw[1]], [], raw[1])
            pnorm(kb, W, raw[0], qn, S, 1.0 / 128, pc[:, PC["fwq"]:PC["fwq"] + 1], 1.0)
            pnorm(kb, W, raw[1], kn, S, 1.0 / 128, pc[:, PC["fwk"]:PC["fwk"] + 1], 1.0)
            kb.dma("sp", raw[0][:], proj_d[(EC_FV + h) * 128:(EC_FV + h + 1) * 128, 0:S], [raw[0]], [], raw[0])
            kb.dma("sp", raw[1][:], proj_d[(EC_FZ + h) * 128:(EC_FZ + h + 1) * 128, 0:S], [raw[1]], [], raw[1])
            for b0 in range(0, nblk, 4):
                nv = min(4, nblk - b0)
                for j in range(nv):
                    kb.op("pe", lambda e: e.transpose(out=ptr[:, j * 128:(j + 1) * 128],
                                                      in_=raw[0][:, (b0 + j) * 128:(b0 + j + 1) * 128],
                                                      identity=C["ident_f"][:]),
                          reads=[raw[0], C["ident_f"]], writes=[ptr], inc=(j == nv - 1))
                kb.op("act", lambda e: e.activation(out=vtok[:, b0:b0 + nv, 0:128],
                                                    in_=ptr[:, 0:nv * 128].rearrange("p (a b) -> p a b", a=nv), func=AF.Copy),
                      reads=[ptr], writes=[vtok])
            kb.op("act", lambda e: e.activation(out=sz[:], in_=raw[1][:], func=AF.Silu), reads=[raw[1]], writes=[sz])

        def qtiles(X, h):
            qn, kn, vtok, sz, mixo = X["qn"], X["kn"], X["vtok"], X["sz"], X["mixo"]
            o_ = X["po"]
            ns = 0
            for i in range(nblk):
                bi = X["bias"][i % 2]
                kb.op("dve", lambda e: e.tensor_scalar(out=bi[:], in0=Fcol[:, :, h], scalar1=-1.0,
                                                       scalar2=Fmid[:, i, h:h + 1], op0=ALU.mult, op1=ALU.add),
                      reads=[Fcol, Fmid], writes=[bi])
                for j0 in range(0, i + 1, 4):
                    nj = min(4, i + 1 - j0)
                    s_ = X["pss"][ns % 2]
                    p_ = X["pt"][ns % 3]
                    ns += 1
                    for jj in range(nj):
                        j = j0 + jj
                        kb.op("pe", lambda e: e.matmul(s_[:, jj * 128:(jj + 1) * 128], lhsT=kn[:, j * 128:(j + 1) * 128],
                                                       rhs=qn[:, i * 128:(i + 1) * 128], start=True, stop=True),
                              reads=[kn, qn], writes=[s_], inc=(jj == nj - 1))
                    yield
                    for jj in range(nj):
                        j = j0 + jj
                        kb.op("act", lambda e: e.activation(out=p_[:, jj, :], in_=s_[:, jj * 128:(jj + 1) * 128],
                                                            func=AF.Exp, scale=sc, bias=bi[:, j:j + 1]),
                              reads=[s_, bi], writes=[p_])
                        if j == i:
                            kb.op("dve", lambda e: e.tensor_tensor(out=p_[:, jj, :], in0=p_[:, jj, :],
                                                                   in1=C["tri_bf"][:], op=ALU.mult),
                                  reads=[p_, C["tri_bf"]], writes=[p_])
                    yield
                    for jj in range(nj):
                        j = j0 + jj
                        kb.op("pe", lambda e: e.matmul(o_[:, 0:129], lhsT=p_[:, jj, :], rhs=vtok[:, j, 0:129],
                                                       start=(j == 0), stop=(j == i)),
                              reads=[p_, vtok], writes=[o_], inc=(jj == nj - 1))
                    yield
                c_ = X["col"][i % 2]
                ob = X["osb"][i % 2]
                kb.op("dve", lambda e: e.reciprocal(out=c_[:, 0:1], in_=o_[:, 128:129]), reads=[o_], writes=[c_])
                kb.op("dve", lambda e: e.tensor_scalar(out=ob[:], in0=o_[:, 0:128], scalar1=c_[:, 0:1], scalar2=None,
                                                       op0=ALU.mult), reads=[o_, c_], writes=[ob])
                yield
                kb.op("act", lambda e: e.activation(out=X["junk"][:], in_=ob[:], func=AF.Square, accum_out=c_[:, 1:2]),
                      reads=[ob], writes=[X["junk"], c_])
                kb.op("act", lambda e: e.activation(out=c_[:, 2:3], in_=c_[:, 1:2], func=AF.Ln, scale=1.0 / 128,
                                                    bias=C["eps"][:]), reads=[c_, C["eps"]], writes=[c_])
                yield
                kb.op("act", lambda e: e.activation(out=c_[:, 3:4], in_=c_[:, 2:3], func=AF.Exp, scale=-0.5), reads=[c_], writes=[c_])
                kb.op("dve", lambda e: e.tensor_scalar(out=ob[:], in0=ob[:], scalar1=c_[:, 3:4], scalar2=None,
                                                       op0=ALU.mult), reads=[ob, c_], writes=[ob])
                yield
                kb.op("pe", lambda e: e.transpose(out=ptr[:, 0:128], in_=ob[:], identity=C["ident_f"][:]),
                      reads=[ob, C["ident_f"]], writes=[ptr])
                kb.op("dve", lambda e: e.scalar_tensor_tensor(out=mixo[:, i * 128:(i + 1) * 128], in0=ptr[:, 0:128],
                                                              scalar=pc[:, PC["fwo"]:PC["fwo"] + 1],
                                                              in1=sz[:, i * 128:(i + 1) * 128],
                                                              op0=ALU.mult, op1=ALU.mult),
                      reads=[ptr, pc, sz], writes=[mixo])
                yield
            kb.dma("sp", mix_d[:, h * 128:(h + 1) * 128, :].rearrange("n p t -> p n t"),
                   mixo[:].rearrange("p (n t) -> p n t", t=mix_th(S)), [Buf(None, "m")], [mixo], mixo)

        for hp in range(2):
            for sl in range(2):
                setup(slots[sl], 2 * hp + sl)
            run_interleaved([qtiles(slots[sl], 2 * hp + sl) for sl in range(2)])
        kb.barrier()
        kb.release_dsems()


def stage_ssd(kb, W, T, S, proj_d, mix_d):
    C = W["C"]
    pc = W["pc"]
    pr = W["pr"]
    nblk = S // 128
    rate, spt = T["rate"], T["sp"]
    with ExitStack() as st:
        W["craw"] = kb.sb(st, [128, S + 3], F32, "craw")
        kb.op("dve", lambda e: e.memset(W["craw"][:, 0:3], 0.0), reads=[], writes=[W["craw"]])
        xT = [kb.sb(st, [128, S], F32, "sxT") for _ in range(3)]
        W["cacc"] = xT[0]
        zsb = [kb.sb(st, [128, 3, 128], F32, "szs") for _ in range(2)]
        BTb = kb.sb(st, [128, S], BF16, "BTb")
        CTb = kb.sb(st, [128, S], BF16, "CTb")
        hst = kb.sb(st, [128, 384], F32, "hst")
        hbf = kb.sb(st, [128, 384], BF16, "hbf")
        R2 = range(2)
        xtok = [kb.sb(st, [128, 384], F32, "xtok") for _ in R2]
        xdt = [kb.sb(st, [128, 384], BF16, "xdt") for _ in R2]
        Btok = [kb.sb(st, [128, 128], BF16, "Btok") for _ in R2]
        at = [kb.sb(st, [128, 6, 128], F32, "at") for _ in R2]
        LT = [kb.sb(st, [128, 6, 128], F32, "LT") for _ in R2]
        MT = [kb.sb(st, [128, 6, 128], BF16, "MT") for _ in R2]
        gts = [kb.sb(st, [128, 128], F32, "gts") for _ in R2]
        ecol = [kb.sb(st, [128, 12], F32, "ecol") for _ in R2]
        t1 = [kb.sb(st, [128, 384], F32, "t1") for _ in R2]
        t2 = [kb.sb(st, [128, 384], F32, "t2") for _ in R2]
        ytok = [kb.sb(st, [128, 384], F32, "ytok") for _ in R2]
        xw = [kb.sb(st, [128, 384], BF16, "xw") for _ in R2]
        gsb = [kb.sb(st, [128, 3, 128], F32, "gsb") for _ in R2]
        sqb = [kb.sb(st, [128, 384], F32, "ssq") for _ in R2]
        rtt = [kb.sb(st, [128, 128], F32, "rtt") for _ in R2]
        osb = [kb.sb(st, [128, 3, 128], MIXDT, "sosb") for _ in R2]
        pn0, pn1 = W["pn"]
        pD0 = kb.ps(st, [128, 512], F32, "pD0")
        pD1 = kb.ps(st, [128, 512], F32, "pD1")
        pY = kb.ps(st, [128, 512], F32, "pY")
        pYo = kb.ps(st, [128, 512], F32, "pYo")
        pS = kb.ps(st, [128, 512], F32, "pS")
        pT = kb.ps(st, [128, 512], F32, "pT")

        for g in range(2):
            conv_silu(kb, W, proj_d, EC_SB + g, PC["sconv"] + 4 * (6 + g), PC["sconvb"] + 6 + g, BTb, S, False)
            conv_silu(kb, W, proj_d, EC_SC + g, PC["sconv"] + 4 * (8 + g), PC["sconvb"] + 8 + g, CTb, S, False)
            zreg = Buf(None, "zreg")
            for k3 in range(3):
                i = g * 3 + k3
                conv_silu(kb, W, proj_d, EC_SX + i, PC["sconv"] + 4 * i, PC["sconvb"] + i, xT[k3], S, True)
                zrows = proj_d[(EC_SZ + i) * 128:(EC_SZ + i + 1) * 128, 0:S]
                kb.dma("sp", W["craw"][:, 3:3 + S], zrows, [W["craw"]], [], W["craw"])
                kb.op("act", lambda e: e.activation(out=W["craw"][:, 3:3 + S], in_=W["craw"][:, 3:3 + S], func=AF.Silu),
                      reads=[W["craw"]], writes=[W["craw"]])
                kb.dma("sp", zrows, W["craw"][:, 3:3 + S], [zreg], [W["craw"]], W["craw"])
            kb.op("dve", lambda e: e.memset(hst[:], 0.0), reads=[], writes=[hst])
            kb.op("dve", lambda e: e.memset(hbf[:], 0.0), reads=[], writes=[hbf])
            c0 = 4 + g * 6
            def Pgen(c):
                    u = c % 2
                    blk = slice(c * 128, (c + 1) * 128)
                    a6 = rate[:, c, c0:c0 + 6]
                    zr0 = (EC_SZ + g * 3) * 128
                    kb.dma("sp", zsb[u][:], proj_d[zr0:zr0 + 384, blk].rearrange("(k p) l -> p k l", p=128),
                           [zsb[u]], [zreg], zsb[u])
                    dt6 = spt[:, c, c0:c0 + 6]
                    for k3 in range(3):
                        kb.op("pe", lambda e: e.transpose(out=pT[:, k3 * 128:(k3 + 1) * 128], in_=xT[k3][:, blk],
                                                          identity=C["ident_f"][:]),
                              reads=[xT[k3], C["ident_f"]], writes=[pT], inc=(k3 == 2))
                    kb.op("act", lambda e: e.activation(out=xtok[u][:], in_=pT[:, 0:384], func=AF.Copy),
                          reads=[pT], writes=[xtok[u]])
                    kb.op("dve", lambda e: e.tensor_tensor(out=xdt[u][:].rearrange("p (h d) -> p h d", h=6),
                                                           in0=xtok[u][:].rearrange("p (h d) -> p h d", h=6),
                                                           in1=dt6.unsqueeze(2).broadcast_to([128, 6, 64]), op=ALU.mult),
                          reads=[xtok[u], spt], writes=[xdt[u]])
                    kb.op("pe", lambda e: e.matmul(pn1[:, 128:256], lhsT=BTb[:, blk], rhs=C["ident_bf"][:], start=True, stop=True),
                          reads=[BTb, C["ident_bf"]], writes=[pn1])
                    kb.op("act", lambda e: e.activation(out=Btok[u][:], in_=pn1[:, 128:256], func=AF.Copy),
                          reads=[pn1], writes=[Btok[u]])
                    yield
                    kb.op("pool", lambda e: e.tensor_tensor(out=at[u][:], in0=C["tri_f"][:].unsqueeze(1).broadcast_to([128, 6, 128]),
                                                           in1=a6.unsqueeze(2).broadcast_to([128, 6, 128]), op=ALU.mult),
                          reads=[C["tri_f"], rate], writes=[at[u]])
                    kb.op("pe", lambda e: e.matmul(pD0[:], lhsT=C["striT"][:], rhs=at[u][:, 0:4, :].rearrange("p a b -> p (a b)"),
                                                   start=True, stop=False), reads=[C["striT"], at[u]], writes=[pD0], inc=False)
                    kb.op("pe", lambda e: e.matmul(pD0[:], lhsT=C["ident_f"][:], rhs=C["negm_ns4"][:], start=False, stop=True),
                          reads=[C["ident_f"], C["negm_ns4"]], writes=[pD0])
                    kb.op("pe", lambda e: e.matmul(pD1[:, 0:256], lhsT=C["striT"][:], rhs=at[u][:, 4:6, :].rearrange("p a b -> p (a b)"),
                                                   start=True, stop=False), reads=[C["striT"], at[u]], writes=[pD1], inc=False)
                    kb.op("pe", lambda e: e.matmul(pD1[:, 0:256], lhsT=C["ident_f"][:], rhs=C["negm_ns4"][:, 0:256], start=False, stop=True),
                          reads=[C["ident_f"], C["negm_ns4"]], writes=[pD1])
                    kb.op("act", lambda e: e.activation(out=LT[u][:, 0:4, :].rearrange("p a b -> p (a b)"), in_=pD0[:], func=AF.Exp),
                          reads=[pD0], writes=[LT[u]])
                    kb.op("act", lambda e: e.activation(out=LT[u][:, 4:6, :].rearrange("p a b -> p (a b)"), in_=pD1[:, 0:256], func=AF.Exp),
                          reads=[pD1], writes=[LT[u]])
                    yield
                    kb.op("pe", lambda e: e.matmul(pn1[:, 0:128], lhsT=BTb[:, blk], rhs=CTb[:, blk], start=True, stop=True),
                          reads=[BTb, CTb], writes=[pn1])
                    kb.op("act", lambda e: e.activation(out=gts[u][:], in_=pn1[:, 0:128], func=AF.Copy), reads=[pn1], writes=[gts[u]])
                    kb.op("dve", lambda e: e.tensor_tensor(out=MT[u][:], in0=LT[u][:],
                                                           in1=gts[u][:].unsqueeze(1).broadcast_to([128, 6, 128]), op=ALU.mult),
                          reads=[LT[u], gts[u]], writes=[MT[u]])
                    yield
                    kb.op("pe", lambda e: e.matmul(pn0[:, 0:6], lhsT=C["tri_f"][:], rhs=a6, start=True, stop=True),
                          reads=[C["tri_f"], rate], writes=[pn0], inc=False)
                    kb.op("pe", lambda e: e.matmul(pn0[:, 6:12], lhsT=C["ones_f"][:], rhs=a6, start=True, stop=True),
                          reads=[C["ones_f"], rate], writes=[pn0])
                    kb.op("act", lambda e: e.activation(out=ecol[u][:], in_=pn0[:, 0:12], func=AF.Exp), reads=[pn0], writes=[ecol[u]])

                    yield

            def Dgen(c):
                    u = c % 2
                    blk = slice(c * 128, (c + 1) * 128)
                    a6 = rate[:, c, c0:c0 + 6]
                    for h in range(6):
                        kb.op("pe", lambda e: e.matmul(pY[:, h * 64:(h + 1) * 64], lhsT=MT[u][:, h, :],
                                                       rhs=xdt[u][:, h * 64:(h + 1) * 64], start=True, stop=True),
                              reads=[MT[u], xdt[u]], writes=[pY], inc=(h == 5))
                    kb.op("pe", lambda e: e.matmul(pYo[:, 0:384], lhsT=CTb[:, blk], rhs=hbf[:], start=True, stop=True),
                          reads=[CTb, hbf], writes=[pYo])
                    kb.op("dve", lambda e: e.tensor_tensor(out=t1[u][:].rearrange("p (h d) -> p h d", h=6),
                                                           in0=pYo[:, 0:384].rearrange("p (h d) -> p h d", h=6),
                                                           in1=ecol[u][:, 0:6].unsqueeze(2).broadcast_to([128, 6, 64]), op=ALU.mult),
                          reads=[pYo, ecol[u]], writes=[t1[u]])
                    kb.op("dve", lambda e: e.tensor_tensor(out=ytok[u][:], in0=pY[:, 0:384], in1=t1[u][:], op=ALU.add),
                          reads=[pY, t1[u]], writes=[ytok[u]])
                    kb.op("pool", lambda e: e.tensor_tensor(out=t2[u][:], in0=xtok[u][:],
                                                           in1=pr[:, 3 * SMW + g * 384:3 * SMW + (g + 1) * 384], op=ALU.mult),
                          reads=[xtok[u], pr], writes=[t2[u]])
                    kb.op("dve", lambda e: e.tensor_tensor(out=ytok[u][:], in0=ytok[u][:], in1=t2[u][:], op=ALU.add),
                          reads=[ytok[u], t2[u]], writes=[ytok[u]])
                    yield
                    kb.op("dve", lambda e: e.tensor_tensor(out=xw[u][:].rearrange("p (h d) -> p h d", h=6),
                                                           in0=xdt[u][:].rearrange("p (h d) -> p h d", h=6),
                                                           in1=LT[u][:, :, 127:128].broadcast_to([128, 6, 64]), op=ALU.mult),
                          reads=[xdt[u], LT[u]], writes=[xw[u]])
                    kb.op("pe", lambda e: e.matmul(pS[:, 0:384], lhsT=Btok[u][:], rhs=xw[u][:], start=True, stop=True),
                          reads=[Btok[u], xw[u]], writes=[pS])
                    kb.op("dve", lambda e: e.tensor_tensor(out=hst[:].rearrange("p (h d) -> p h d", h=6),
                                                           in0=hst[:].rearrange("p (h d) -> p h d", h=6),
                                                           in1=ecol[u][:, 6:12].unsqueeze(2).broadcast_to([128, 6, 64]), op=ALU.mult),
                          reads=[hst, ecol[u]], writes=[hst])
                    kb.op("dve", lambda e: e.tensor_tensor(out=hst[:], in0=hst[:], in1=pS[:, 0:384], op=ALU.add),
                          reads=[hst, pS], writes=[hst])
                    kb.op("act", lambda e: e.activation(out=hbf[:], in_=hst[:], func=AF.Copy), reads=[hst], writes=[hbf])
                    yield
                    for k3 in range(3):
                        kb.op("pe", lambda e: e.transpose(out=pT[:, k3 * 128:(k3 + 1) * 128], in_=ytok[u][:, k3 * 128:(k3 + 1) * 128],
                                                          identity=C["ident_f"][:]),
                              reads=[ytok[u], C["ident_f"]], writes=[pT], inc=(k3 == 2))
                    for k3 in range(3):
                        kb.op("dve", lambda e: e.tensor_tensor(out=gsb[u][:, k3, :], in0=pT[:, k3 * 128:(k3 + 1) * 128],
                                                               in1=zsb[u][:, k3, :], op=ALU.mult),
                              reads=[pT, zsb[u]], writes=[gsb[u]])
                    yield
                    kb.op("act", lambda e: e.activation(out=sqb[u][:], in_=gsb[u][:].rearrange("p a b -> p (a b)"), func=AF.Square),
                          reads=[gsb[u]], writes=[sqb[u]])
                    for k3 in range(3):
                        kb.op("pe", lambda e: e.matmul(pn0[:, 128:256], lhsT=C["ones_f"][:], rhs=sqb[u][:, k3 * 128:(k3 + 1) * 128],
                                                       start=(k3 == 0), stop=(k3 == 2)),
                              reads=[C["ones_f"], sqb[u]], writes=[pn0], inc=(k3 == 2))
                    kb.op("act", lambda e: e.activation(out=rtt[u][:], in_=pn0[:, 128:256], func=AF.Ln, scale=1.0 / 384,
                                                        bias=C["eps"][:]), reads=[pn0, C["eps"]], writes=[rtt[u]])
                    kb.op("act", lambda e: e.activation(out=rtt[u][:], in_=rtt[u][:], func=AF.Exp, scale=-0.5), reads=[rtt[u]], writes=[rtt[u]])
                    for k3 in range(3):
                        wc = PC["snw"] + g * 3 + k3
                        kb.op("dve", lambda e: e.scalar_tensor_tensor(out=osb[u][:, k3, :], in0=gsb[u][:, k3, :],
                                                                      scalar=pc[:, wc:wc + 1], in1=rtt[u][:],
                                                                      op0=ALU.mult, op1=ALU.mult),
                              reads=[gsb[u], pc, rtt[u]], writes=[osb[u]])
                    r0 = 512 + g * 384
                    mth, moff = (c * 128) // mix_th(S), (c * 128) % mix_th(S)
                    kb.dma("sp", mix_d[mth, r0:r0 + 384, moff:moff + 128].rearrange("(k p) l -> p k l", p=128), osb[u][:],
                           [Buf(None, "m")], [osb[u]], osb[u])

                    yield

            run_interleaved([Pgen(0)])
            for c in range(nblk):
                gl = [Dgen(c)]
                if c + 1 < nblk:
                    gl.insert(0, Pgen(c + 1))
                run_interleaved(gl)
        kb.barrier()
        kb.release_dsems()


def stage_gdn(kb, W, T, S, proj_d, mix_d):
    C = W["C"]
    pc = W["pc"]
    nblk = S // 128
    rate, betaT = T["rate"], T["beta"]
    idf = C["ident_f"]
    with ExitStack() as st:
        W["craw"] = kb.sb(st, [128, S + 3], F32, "craw")
        kb.op("dve", lambda e: e.memset(W["craw"][:, 0:3], 0.0), reads=[], writes=[W["craw"]])
        tmp = kb.sb(st, [128, S], F32, "gtmp")
        W["cacc"] = tmp
        vTb = kb.sb(st, [128, S], BF16, "gvT")
        g6 = kb.sb(st, [128, nblk, 6], F32, "g6")
        gc = kb.sb(st, [128, nblk, 6], F32, "gc")
        tot = kb.sb(st, [128, nblk, 6], F32, "tot")
        kbs = kb.sb(st, [128, nblk, 6], F32, "kbs")
        etl = kb.sb(st, [128, nblk, 6], F32, "etl")
        els = kb.sb(st, [128, nblk, 6], F32, "els")
        banks = [kb.ps(st, [128, 512], F32, "gbank") for _ in range(8)]
        R2 = range(2)
        slots = []
        for sl in range(2):
            X = {}
            X["qT"] = kb.sb(st, [128, S], BF16, "gqT")
            X["kTb"] = kb.sb(st, [128, S], BF16, "gkT")
            X["mixo"] = kb.sb(st, [128, S], MIXDT, "gmixo")
            X["Kbg"] = kb.sb(st, [128, nblk, 128], BF16, "Kbg")
            X["Ktl"] = kb.sb(st, [128, nblk, 128], BF16, "Ktl")
            X["Vb"] = kb.sb(st, [128, nblk, 128], BF16, "Vb")
            X["Sst"] = kb.sb(st, [128, 128], F32, "Sst")
            X["Sbf"] = kb.sb(st, [128, 128], BF16, "Sbf")
            X["at"] = [kb.sb(st, [128, 128], F32, "gat") for _ in R2]
            X["dl"] = [kb.sb(st, [128, 128], F32, "gdl") for _ in R2]
            X["E124"] = [kb.sb(st, [128, 384], F32, "E124") for _ in R2]
            X["E3"] = [kb.sb(st, [128, 128], F32, "E3") for _ in R2]
            X["XX"] = [kb.sb(st, [128, 256], F32, "XX") for _ in range(3)]
            X["Pm"] = [kb.sb(st, [128, 128], F32, "Pm") for _ in range(3)]
            X["TTb"] = [kb.sb(st, [128, 128], BF16, "TTb") for _ in R2]
            X["qkT"] = [kb.sb(st, [128, 128], BF16, "qkT") for _ in R2]
            X["qdT"] = [kb.sb(st, [128, 128], BF16, "qdT") for _ in R2]
            X["wTb"] = [kb.sb(st, [128, 128], BF16, "wTb") for _ in R2]
            X["usb"] = [kb.sb(st, [128, 128], F32, "usb") for _ in R2]
            X["vnb"] = [kb.sb(st, [128, 128], BF16, "vnb") for _ in R2]
            X["osb"] = [kb.sb(st, [128, 128], F32, "gosb") for _ in R2]
            X["zsb"] = [kb.sb(st, [128, 128], F32, "gzsb") for _ in R2]
            X["col"] = [kb.sb(st, [128, 4], F32, "gcol") for _ in R2]
            X["junk"] = kb.sb(st, [128, 128], F32, "gjunk")
            bA, bB, bC, bD = banks[4 * sl:4 * sl + 4]
            X["banks"] = (bA, bB, bC, bD)
            X["r124"] = Buf(bA.t[:, 0:384], "r124", share=bA)
            X["r3"] = Buf(bA.t[:, 384:512], "r3", share=bA)
            X["kq"] = Buf(bB.t[:, 0:256], "kq", share=bB)
            X["n2"] = Buf(bB.t[:, 256:512], "n2", share=bB)
            X["pP"] = Buf(bC.t[:, 0:128], "pP", share=bC)
            X["pWt"] = Buf(bC.t[:, 128:256], "pWt", share=bC)
            X["pU"] = Buf(bC.t[:, 256:384], "pU", share=bC)
            X["pWS"] = Buf(bC.t[:, 384:512], "pWS", share=bC)
            X["pO"] = Buf(bD.t[:, 0:128], "pO", share=bD)
            X["pOT"] = Buf(bD.t[:, 128:256], "pOT", share=bD)
            X["pSn"] = Buf(bD.t[:, 256:384], "pSn", share=bD)
            slots.append(X)

        pn0 = banks[0]
        kb.op("dve", lambda e: e.tensor_copy(out=g6[:], in_=rate[:, :, 22:28]), reads=[rate], writes=[g6])
        g6f = g6[:].rearrange("p a b -> p (a b)")
        n6 = nblk * 6
        kb.op("pe", lambda e: e.matmul(pn0[:, 0:n6], lhsT=C["tri_f"][:], rhs=g6f, start=True, stop=True),
              reads=[C["tri_f"], g6], writes=[pn0])
        kb.op("dve", lambda e: e.tensor_copy(out=gc[:].rearrange("p a b -> p (a b)"), in_=pn0[:, 0:n6]), reads=[pn0], writes=[gc])
        kb.op("pe", lambda e: e.matmul(pn0[:, 0:n6], lhsT=C["ones_f"][:], rhs=g6f, start=True, stop=True),
              reads=[C["ones_f"], g6], writes=[pn0])
        kb.op("dve", lambda e: e.tensor_copy(out=tot[:].rearrange("p a b -> p (a b)"), in_=pn0[:, 0:n6]), reads=[pn0], writes=[tot])
        kb.op("dve", lambda e: e.tensor_tensor(out=etl[:], in0=tot[:], in1=gc[:], op=ALU.subtract), reads=[tot, gc], writes=[etl])
        kb.op("act", lambda e: e.activation(out=etl[:], in_=etl[:], func=AF.Exp), reads=[etl], writes=[etl])
        kb.op("act", lambda e: e.activation(out=els[:], in_=tot[:], func=AF.Exp), reads=[tot], writes=[els])
        kb.op("act", lambda e: e.activation(out=kbs[:], in_=gc[:], func=AF.Exp), reads=[gc], writes=[kbs])
        kb.op("dve", lambda e: e.tensor_tensor(out=kbs[:], in0=kbs[:], in1=betaT[:, :, 16:22], op=ALU.mult),
              reads=[kbs, betaT], writes=[kbs])
        kb.barrier()

        def setup(X, h):
            bA, bB, bE, bF = X["banks"]
            W["pn"] = [bA, bB]
            qT, kTb = X["qT"], X["kTb"]
            conv_silu(kb, W, proj_d, EC_GQ + h, PC["gconv"] + 4 * h, None, tmp, S, True)
            pnorm(kb, W, tmp, qT, S, 1.0, None, 128.0 ** -0.5)
            conv_silu(kb, W, proj_d, EC_GK + h, PC["gconv"] + 4 * (6 + h), None, tmp, S, True)
            pnorm(kb, W, tmp, kTb, S, 1.0, None, 1.0)
            conv_silu(kb, W, proj_d, EC_GV + h, PC["gconv"] + 4 * (12 + h), None, vTb, S, False)
            zrows = proj_d[(EC_GZ + h) * 128:(EC_GZ + h + 1) * 128, 0:S]
            kb.dma("sp", W["craw"][:, 3:3 + S], zrows, [W["craw"]], [], W["craw"])
            kb.op("act", lambda e: e.activation(out=W["craw"][:, 3:3 + S], in_=W["craw"][:, 3:3 + S], func=AF.Silu),
                  reads=[W["craw"]], writes=[W["craw"]])
            kb.dma("sp", zrows, W["craw"][:, 3:3 + S], [Buf(None, "zreg")], [W["craw"]], W["craw"])
            for b0 in range(0, nblk, 4):
                nb = min(4, nblk - b0)
                for j in range(nb):
                    sl_ = slice((b0 + j) * 128, (b0 + j + 1) * 128)
                    kb.op("pe", lambda e: e.matmul(bE[:, j * 128:(j + 1) * 128], lhsT=kTb[:, sl_], rhs=C["ident_bf"][:],
                                                   start=True, stop=True), reads=[kTb, C["ident_bf"]], writes=[bE], inc=False)
                    kb.op("pe", lambda e: e.matmul(bF[:, j * 128:(j + 1) * 128], lhsT=vTb[:, sl_], rhs=C["ident_bf"][:],
                                                   start=True, stop=True), reads=[vTb, C["ident_bf"]], writes=[bF], inc=(j == nb - 1))

                def bcs(t, cc):
                    return t[:, b0:b0 + nb, cc:cc + 1].broadcast_to([128, nb, 128])
                kview = bE[:, 0:nb * 128].rearrange("p (a b) -> p a b", a=nb)
                vview = bF[:, 0:nb * 128].rearrange("p (a b) -> p a b", a=nb)
                kb.op("dve", lambda e: e.tensor_tensor(out=X["Kbg"][:, b0:b0 + nb, :], in0=kview, in1=bcs(kbs, h), op=ALU.mult),
                      reads=[bE, kbs], writes=[X["Kbg"]])
                kb.op("dve", lambda e: e.tensor_tensor(out=X["Ktl"][:, b0:b0 + nb, :], in0=kview, in1=bcs(etl, h), op=ALU.mult),
                      reads=[bE, etl], writes=[X["Ktl"]])
                kb.op("dve", lambda e: e.tensor_tensor(out=X["Vb"][:, b0:b0 + nb, :], in0=vview, in1=bcs(betaT, 16 + h), op=ALU.mult),
                      reads=[bF, betaT], writes=[X["Vb"]])
            kb.op("dve", lambda e: e.memset(X["Sst"][:], 0.0), reads=[], writes=[X["Sst"]])
            kb.op("dve", lambda e: e.memset(X["Sbf"][:], 0.0), reads=[], writes=[X["Sbf"]])

        def chunks(X, h):
            qT, kTb, mixo = X["qT"], X["kTb"], X["mixo"]
            Sst, Sbf = X["Sst"], X["Sbf"]
            r124, r3, kq, n2, pP, pWt, pU, pWS, pO, pOT, pSn = (X[k] for k in
                                                                 ("r124", "r3", "kq", "n2", "pP", "pWt", "pU", "pWS", "pO", "pOT", "pSn"))

            def mm(out, l_, r_, s0, s1, rd, wr, inc=False):
                kb.op("pe", lambda e: e.matmul(out, lhsT=l_, rhs=r_, start=s0, stop=s1), reads=rd, writes=[wr], inc=inc)
            def Pgen(c):
                u = c % 2
                blk = slice(c * 128, (c + 1) * 128)
                gcol = rate[:, c, 22 + h:23 + h]
                lnb = rate[:, c, 16 + h:17 + h]
                at, dl, E124, E3 = X["at"][u], X["dl"][u], X["E124"][u], X["E3"][u]
                zsb = X["zsb"][u]
                zr0 = (EC_GZ + h) * 128
                kb.dma("sp", zsb[:], proj_d[zr0:zr0 + 128, blk], [zsb], [], zsb)
                kb.op("dve", lambda e: e.tensor_scalar(out=at[:], in0=C["tri_f"][:], scalar1=gcol, scalar2=None, op0=ALU.mult),
                      reads=[C["tri_f"], rate], writes=[at])
                kb.op("dve", lambda e: e.tensor_scalar(out=dl[:], in0=idf[:], scalar1=lnb, scalar2=None, op0=ALU.mult),
                      reads=[idf, rate], writes=[dl])
                yield
                mm(r124[:, 0:128], C["striT"][:], at[:], True, False, [C["striT"], at], r124)
                mm(r124[:, 0:128], idf[:], C["negm_ns"][:], False, True, [idf, C["negm_ns"]], r124)
                mm(r124[:, 128:256], C["striT"][:], at[:], True, False, [C["striT"], at], r124)
                mm(r124[:, 128:256], C["ones_f"][:], dl[:], False, False, [C["ones_f"], dl], r124)
                mm(r124[:, 128:256], idf[:], C["negm_s"][:], False, True, [idf, C["negm_s"]], r124)
                mm(r124[:, 256:384], C["ones_f"][:], at[:], True, True, [C["ones_f"], at], r124, True)
                mm(r3[:], at[:], C["striT"][:], True, False, [C["striT"], at], r3)
                mm(r3[:], idf[:], C["negm_sT"][:], False, True, [idf, C["negm_sT"]], r3, True)
                mm(kq[:, 0:128], kTb[:, blk], kTb[:, blk], True, True, [kTb], kq)
                mm(kq[:, 128:256], kTb[:, blk], qT[:, blk], True, True, [kTb, qT], kq, True)
                yield
                kb.op("act", lambda e: e.activation(out=E124[:], in_=r124[:], func=AF.Exp), reads=[r124], writes=[E124])
                kb.op("act", lambda e: e.activation(out=E3[:], in_=r3[:], func=AF.Exp, bias=lnb), reads=[r3, rate], writes=[E3])
                yield
                X0 = X["XX"][0]
                kb.op("dve", lambda e: e.scalar_tensor_tensor(out=X0[:, 0:128].bitcast(F32R), in0=kq[:, 0:128], scalar=-1.0,
                                                              in1=E124[:, 128:256], op0=ALU.mult, op1=ALU.mult),
                      reads=[kq, E124], writes=[X0])
                kb.op("dve", lambda e: e.scalar_tensor_tensor(out=X0[:, 128:256].bitcast(F32R), in0=kq[:, 0:128], scalar=-1.0,
                                                              in1=E3[:], op0=ALU.mult, op1=ALU.mult),
                      reads=[kq, E3], writes=[X0])
                kb.op("dve", lambda e: e.tensor_tensor(out=X["Pm"][0][:].bitcast(F32R), in0=idf[:], in1=X0[:, 0:128], op=ALU.add),
                      reads=[idf, X0], writes=[X["Pm"][0]])
                kb.op("dve", lambda e: e.tensor_tensor(out=X["qkT"][u][:], in0=kq[:, 128:256], in1=E124[:, 0:128], op=ALU.mult),
                      reads=[kq, E124], writes=[X["qkT"][u]])
                kb.op("dve", lambda e: e.tensor_tensor(out=X["qdT"][u][:], in0=qT[:, blk], in1=E124[:, 256:384], op=ALU.mult),
                      reads=[qT, E124], writes=[X["qdT"][u]])
                yield
                xi, pi = 0, 0
                for lev in range(1, 7):
                    Xc, Pc = X["XX"][xi], X["Pm"][pi]
                    Xn = X["XX"][(xi + 1) % 3]
                    if lev < 6:
                        mm(n2[:, 0:128], Xc[:, 128:256].bitcast(F32R), Xc[:, 0:128].bitcast(F32R), True, True, [Xc], n2)
                    mm(n2[:, 128:256], Xc[:, 0:128].bitcast(F32R), Xc[:, 128:256].bitcast(F32R), True, True, [Xc], n2, True)
                    yield
                    if lev < 6:
                        kb.op("act", lambda e: e.activation(out=Xn[:].bitcast(F32R), in_=n2[:], func=AF.Copy), reads=[n2], writes=[Xn])
                    else:
                        kb.op("act", lambda e: e.activation(out=Xn[:, 128:256].bitcast(F32R), in_=n2[:, 128:256], func=AF.Copy),
                              reads=[n2], writes=[Xn])
                    yield
                    mm(pP[:], Xn[:, 128:256].bitcast(F32R), Pc[:].bitcast(F32R), True, True, [Xn, Pc], pP, True)
                    yield
                    if lev < 6:
                        Pn = X["Pm"][(pi + 1) % 3]
                        kb.op("dve", lambda e: e.tensor_tensor(out=Pn[:].bitcast(F32R), in0=pP[:], in1=Pc[:], op=ALU.add),
                              reads=[pP, Pc], writes=[Pn])
                        pi = (pi + 1) % 3
                    else:
                        kb.op("dve", lambda e: e.tensor_tensor(out=X["TTb"][u][:], in0=pP[:], in1=Pc[:], op=ALU.add),
                              reads=[pP, Pc], writes=[X["TTb"][u]])
                    xi = (xi + 1) % 3
                    yield
                TTb, wTb, usb, vnb = X["TTb"][u], X["wTb"][u], X["usb"][u], X["vnb"][u]
                mm(pWt[:], X["Kbg"][:, c, :], TTb[:], True, True, [X["Kbg"], TTb], pWt, True)
                mm(pU[:], TTb[:], X["Vb"][:, c, :], True, True, [TTb, X["Vb"]], pU, True)
                yield
                kb.op("act", lambda e: e.activation(out=wTb[:], in_=pWt[:], func=AF.Copy), reads=[pWt], writes=[wTb])
                kb.op("act", lambda e: e.activation(out=usb[:], in_=pU[:], func=AF.Copy), reads=[pU], writes=[usb])
                yield
                yield

            def Dgen(c):
                u = c % 2
                blk = slice(c * 128, (c + 1) * 128)
                zsb = X["zsb"][u]
                TTb, wTb, usb, vnb = X["TTb"][u], X["wTb"][u], X["usb"][u], X["vnb"][u]
                mm(pWS[:], wTb[:], Sbf[:], True, True, [wTb, Sbf], pWS, True)
                yield
                kb.op("dve", lambda e: e.tensor_tensor(out=vnb[:], in0=usb[:], in1=pWS[:], op=ALU.subtract),
                      reads=[usb, pWS], writes=[vnb])
                yield
                mm(pO[:], X["qdT"][u][:], Sbf[:], True, False, [X["qdT"][u], Sbf], pO)
                mm(pO[:], X["qkT"][u][:], vnb[:], False, True, [X["qkT"][u], vnb], pO, True)
                mm(pSn[:], X["Ktl"][:, c, :], vnb[:], True, True, [X["Ktl"], vnb], pSn, True)
                yield
                kb.op("dve", lambda e: e.scalar_tensor_tensor(out=Sst[:], in0=Sst[:], scalar=els[:, c, h:h + 1], in1=pSn[:],
                                                              op0=ALU.mult, op1=ALU.add), reads=[Sst, els, pSn], writes=[Sst])
                kb.op("act", lambda e: e.activation(out=Sbf[:], in_=Sst[:], func=AF.Copy), reads=[Sst], writes=[Sbf])
                c_ = X["col"][u]
                osb = X["osb"][u]
                kb.op("act", lambda e: e.activation(out=X["junk"][:], in_=pO[:], func=AF.Square, accum_out=c_[:, 0:1]),
                      reads=[pO], writes=[X["junk"], c_])
                kb.op("act", lambda e: e.activation(out=c_[:, 1:2], in_=c_[:, 0:1], func=AF.Ln, scale=1.0 / 128,
                                                    bias=C["eps"][:]), reads=[c_, C["eps"]], writes=[c_])
                yield
                kb.op("act", lambda e: e.activation(out=c_[:, 2:3], in_=c_[:, 1:2], func=AF.Exp, scale=-0.5), reads=[c_], writes=[c_])
                kb.op("dve", lambda e: e.tensor_scalar(out=osb[:], in0=pO[:], scalar1=c_[:, 2:3], scalar2=None, op0=ALU.mult),
                      reads=[pO, c_], writes=[osb])
                yield
                kb.op("pe", lambda e: e.transpose(out=pOT[:], in_=osb[:], identity=idf[:]), reads=[osb, idf], writes=[pOT])
                yield
                kb.op("dve", lambda e: e.scalar_tensor_tensor(out=mixo[:, blk], in0=pOT[:], scalar=pc[:, PC["gnw"]:PC["gnw"] + 1],
                                                              in1=zsb[:], op0=ALU.mult, op1=ALU.mult),
                      reads=[pOT, pc, zsb], writes=[mixo])
                yield

            yield from Pgen(0)
            for c in range(nblk):
                active = [Dgen(c)]
                if c + 1 < nblk:
                    active.insert(0, Pgen(c + 1))
                while active:
                    for g_ in list(active):
                        try:
                            next(g_)
                            yield
                        except StopIteration:
                            active.remove(g_)
            r0 = 1280 + h * 128
            kb.dma("sp", mix_d[:, r0:r0 + 128, :].rearrange("n p t -> p n t"),
                   mixo[:].rearrange("p (n t) -> p n t", t=mix_th(S)), [Buf(None, "m")], [mixo], mixo)

        for hp in range(3):
            for sl in range(2):
                setup(slots[sl], 2 * hp + sl)
            kb.barrier()
            run_interleaved([chunks(slots[sl], 2 * hp + sl) for sl in range(2)])
            kb.barrier()
        kb.release_dsems()


HD = D // 2


def mix_th(S):
    return min(512, S)


def stage_outproj(kb, S, xres_d, mixg_d, mixg_bufs, wout_d, dst_d):
    TH = min(512, S)
    NDB = HD // 512
    nth = S // TH
    with ExitStack() as st:
        mT = [kb.sb(st, [128, NKC, TH], BF16, "mT") for _ in range(2)]
        wb = [kb.sb(st, [128, NKC, 512], BF16, "wob") for _ in range(2)]
        xt = [kb.sb(st, [128, 512], F32, "xt") for _ in range(3)]
        ot = [kb.sb(st, [128, 512], F32, "ot") for _ in range(3)]
        pp = [kb.ps(st, [128, 512], F32, "pp") for _ in range(3)]

        def load_w(db):
            w_ = wb[db % 2]
            for q4 in range(4):
                kb.dma("pool", w_[:, q4 * 8:(q4 + 1) * 8, :],
                       wout_d[q4 * 1024:(q4 + 1) * 1024, db * 512:(db + 1) * 512].rearrange("(a p) d -> p a d", p=128),
                       [w_], [], w_)

        seq = [(db, th) for db in range(NDB) for th in range(nth)]

        def load_m(k):
            th = seq[k][1]
            m_ = mT[k % 2]
            for q4 in range(4):
                kb.dma("sp", m_[:, q4 * 8:(q4 + 1) * 8, :],
                       mixg_d[th, q4 * 1024:(q4 + 1) * 1024, :].rearrange("(a p) t -> p a t", p=128),
                       [m_], [mixg_bufs[th]], m_)

        load_w(0)
        load_m(0)
        i = 0
        for k, (db, th) in enumerate(seq):
            if th == 0 and db + 1 < NDB:
                load_w(db + 1)
            if k + 1 < len(seq):
                load_m(k + 1)
            w_, m_ = wb[db % 2], mT[k % 2]
            for tb in range(TH // 128):
                t0 = th * TH + tb * 128
                x_, o_, p_ = xt[i % 3], ot[i % 3], pp[i % 3]
                i += 1
                kb.dma("act", x_[:], xres_d[t0:t0 + 128, db * 512:(db + 1) * 512], [x_], [], x_)
                for ec in range(NKC):
                    kb.op("pe", lambda e: e.matmul(p_[:], lhsT=m_[:, ec, tb * 128:(tb + 1) * 128], rhs=w_[:, ec, :],
                                                   start=(ec == 0), stop=(ec == NKC - 1)),
                          reads=[m_, w_], writes=[p_], inc=(ec == NKC - 1))
                kb.op("dve", lambda e: e.tensor_tensor(out=o_[:], in0=p_[:], in1=x_[:], op=ALU.add),
                      reads=[p_, x_], writes=[o_])
                kb.dma("act", dst_d[t0:t0 + 128, db * 512:(db + 1) * 512], o_[:], [Buf(None, "o")], [o_], o_)
        kb.barrier()
        kb.release_dsems()


PAIRS = [[0, 1], [2, 3], [4, 5], [6, 7]]
XCH = 256


def mix_chunk_rows(S):
    return max(1, min(2048, (1 << 21) // (2 * S)))


def build_fused(S, depth=2):
    nc = bass.Bass("TRN2", target_bir_lowering=False)
    kb = KB(nc)
    x_d = nc.dram_tensor("x", [S, D], F32, kind="ExternalInput").ap()
    xres_d = nc.dram_tensor("xres", [S, HD], F32, kind="ExternalInput").ap()
    win_d = nc.dram_tensor("win", [depth, NEC, 128, NKC * 128], F32, kind="ExternalInput").ap()
    nwb_d = nc.dram_tensor("nwb", [depth, 128, D], F32, kind="ExternalInput").ap()
    pc_d = nc.dram_tensor("pc", [depth, 128, NPC], F32, kind="ExternalInput").ap()
    pr_d = nc.dram_tensor("pr", [depth, 128, NPR], F32, kind="ExternalInput").ap()
    wout_d = nc.dram_tensor("wout", [depth, D, HD], F32, kind="ExternalInput").ap()
    out_d = nc.dram_tensor("out", [S, HD], F32, kind="ExternalOutput").ap()
    proj_d = nc.dram_tensor("proj", [NEC * 128, S], F32).ap()
    ca = const_arrays()
    cd = {k: (nc.dram_tensor("c_" + k, list(v.shape), F32, kind="ExternalInput").ap(), list(v.shape),
              CONST_DT.get(k, F32)) for k, v in ca.items()}
    with ExitStack() as st:
        C = load_consts(kb, st, cd)
        xg_d = None
        xh_d = xres_d
        for l in range(depth):
            TM = mix_th(S)
            mixh_d = nc.dram_tensor("mixh%d" % l, [S // TM, 2048, TM], MIXDT).ap()
            mixg_d = nc.dram_tensor("mixg%d" % l, [S // TM, 4096, TM], MIXDT).ap()
            xload = None
            if l > 0:
                xg = xg_d
                xgs = xgs_prev

                def xload(kb_, xs_, j, xg=xg, xgs=xgs):
                    r0 = (j * 128 // XCH) * 2 * XCH + (j * 128) % XCH
                    rb = [xgs[j * 128 // XCH]]
                    kb_.dma("sp", xs_[:, 0:HD], xg[r0:r0 + 128, :], [xs_], rb, xs_)
                    kb_.dma("sp", xs_[:, HD:D], xg[r0 + XCH:r0 + XCH + 128, :], [xs_], rb, xs_)
            stage_inproj(kb, S, x_d, win_d[l], nwb_d[l], proj_d, C, TT=min(512, S), xload=xload)
            with ExitStack() as st2:
                W = alloc_common(kb, st2, S, C, pc_d[l], pr_d[l])
                with ExitStack() as st3:
                    W["pn"] = [kb.ps(st3, [128, 512], F32, "pn") for _ in range(2)]
                    T = stage_small(kb, st2, W, S, proj_d)
                    stage_fox(kb, W, T, S, proj_d, mixh_d)
                    stage_ssd(kb, W, T, S, proj_d, mixh_d)
                stage_gdn(kb, W, T, S, proj_d, mixh_d)
                kb.barrier()
                kb.release_dsems()
            mg = [Buf(None, "mixg") for _ in range(S // TM)]
            for k in range(S // TM):
                kb.collective("AllGather", PAIRS, mixh_d[k], mixg_d[k], Buf(None, "mixh"), mg[k])
            if l + 1 < depth:
                nxh_d = nc.dram_tensor("xh%d" % (l + 1), [S, HD], F32).ap()
                nxg_d = nc.dram_tensor("xg%d" % (l + 1), [2 * S, HD], F32).ap()
                stage_outproj(kb, S, xh_d, mixg_d, mg, wout_d[l], nxh_d)
                xgs = [Buf(None, "xg") for _ in range(S // XCH)]
                for k in range(S // XCH):
                    kb.collective("AllGather", PAIRS, nxh_d[k * XCH:(k + 1) * XCH, :],
                                  nxg_d[k * 2 * XCH:(k + 1) * 2 * XCH, :], Buf(None, "xh"), xgs[k])
                xh_d, xg_d = nxh_d, nxg_d
                xgs_prev = xgs
            else:
                stage_outproj(kb, S, xh_d, mixg_d, mg, wout_d[l], out_d)
        kb.barrier()
    return nc, ca


def mix_rows(hh):
    r = np.arange
    return np.concatenate([hh * 512 + r(512), 1024 + hh * 768 + r(768), 2560 + hh * 768 + r(768)])


def kernel(**inputs):
    inp = {k: np.asarray(v) for k, v in inputs.items()}
    x = np.ascontiguousarray(inp["x"], dtype=np.float32)
    B, S, _ = x.shape
    depth = inp["w_in"].shape[0]
    nc, ca = build_fused(S, depth)
    cores = list(range(8))
    rowperm = np.concatenate([mix_rows(0), mix_rows(1)])
    per_hh = []
    for hh in range(2):
        win = np.stack([prep_win(inp["w_in"][l], hh) for l in range(depth)])
        pps = [prep_params(inp, l, hh) for l in range(depth)]
        pc = np.stack([p[0] for p in pps])
        pr = np.stack([p[1] for p in pps])
        wout = np.stack([np.ascontiguousarray(inp["w_out"][l][rowperm][:, hh * HD:(hh + 1) * HD]) for l in range(depth)])
        per_hh.append((win, pc, pr, wout.astype(np.float32)))
    nwb = np.stack([np.broadcast_to(inp["norm_w"][l].astype(np.float32), (128, D)) for l in range(depth)]).copy()
    maps = []
    for c in cores:
        b, hh = c // 2, c % 2
        win, pc, pr, wout = per_hh[hh]
        m = {"x": x[b], "xres": np.ascontiguousarray(x[b][:, hh * HD:(hh + 1) * HD]), "win": win, "nwb": nwb,
             "pc": pc, "pr": pr, "wout": wout}
        for k, v in ca.items():
            m["c_" + k] = v
        maps.append(m)
    res = run_bass_kernel_spmd(nc, maps, core_ids=cores)
    out = np.empty((B, S, D), np.float32)
    for c in cores:
        b, hh = c // 2, c % 2
        out[b, :, hh * HD:(hh + 1) * HD] = res.results[c]["out"]
    return out
```

```python
import numpy as np
import concourse.bass as bass
import concourse.mybir as mybir
from concourse.bass_utils import run_bass_kernel_spmd
from contextlib import ExitStack

F32 = mybir.dt.float32
BF16 = mybir.dt.bfloat16
F32R = mybir.dt.float32r
AF = mybir.ActivationFunctionType
ALU = mybir.AluOpType
AX = mybir.AxisListType


class Buf:
    def __init__(self, t, name, share=None):
        self.t = t
        self.name = name
        self.s = share.s if share is not None else {"lw": None, "rd": {}}
        self.dkey = None

    @property
    def lw(self):
        return self.s["lw"]

    @lw.setter
    def lw(self, v):
        self.s["lw"] = v

    @property
    def rd(self):
        return self.s["rd"]

    @rd.setter
    def rd(self, v):
        self.s["rd"] = v

    def __getitem__(self, idx):
        return self.t[idx]


class KB:
    def __init__(self, nc):
        self.nc = nc
        self.eng = {"pe": nc.tensor, "act": nc.scalar, "dve": nc.vector,
                    "pool": nc.gpsimd, "sp": nc.sync}
        self.sems = {}
        self.cnt = {}
        for e in ("pe", "act", "dve", "pool"):
            self.sems[e] = nc.alloc_semaphore("s_" + e)
            self.cnt[e] = 0
        self.waited = {}
        self.nid = 0
        self.dbufs = []
        self.free_dkeys = []
        self.dcnt = {}
        self.nwaits = 0
        self.nops = 0

    def uid(self, p):
        self.nid += 1
        return "%s_%d" % (p, self.nid)

    def sb(self, stack, shape, dt, name="sb"):
        t = stack.enter_context(self.nc.sbuf_tensor(self.uid(name), list(shape), dt))
        return Buf(t, name)

    def ps(self, stack, shape, dt, name="ps"):
        t = stack.enter_context(self.nc.psum_tensor(self.uid(name), list(shape), dt))
        return Buf(t, name)

    def dram(self, name, shape, dt, kind=None):
        if kind is None:
            t = self.nc.dram_tensor(name, list(shape), dt)
        else:
            t = self.nc.dram_tensor(name, list(shape), dt, kind=kind)
        return Buf(t.ap(), name)

    def view(self, ap, name="v"):
        return Buf(ap, name)

    def _wait(self, e, key, val):
        k = (e, key)
        if self.waited.get(k, 0) >= val:
            return
        if key == e:
            assert val <= self.cnt[e], "self-wait on pending stamp (%s)" % e
        self.eng[e].wait_ge(self.sems[key], val)
        self.waited[k] = val
        self.nwaits += 1

    def op(self, e, fn, reads=(), writes=(), inc=True):
        for b in reads:
            if b.lw is not None:
                self._wait(e, *b.lw)
        for b in writes:
            if b.lw is not None and (b.lw[0] != e or (e != "pe" and b.lw[1] <= self.cnt[e])):
                self._wait(e, *b.lw)
            for key, val in b.rd.items():
                if key != e or (e != "pe" and val <= self.cnt[e]):
                    self._wait(e, key, val)
        ins = fn(self.eng[e])
        self.nops += 1
        if inc:
            self.cnt[e] += 1
            ins.then_inc(self.sems[e], 1)
            v = self.cnt[e]
        else:
            v = self.cnt[e] + 1
        for b in reads:
            if b.rd.get(e, 0) < v:
                b.rd[e] = v
        for b in writes:
            b.lw = (e, v)
            b.rd = {}
        return ins

    def dma(self, q, out, in_, wbufs, rbufs, owner):
        b = owner
        if b.dkey is None:
            if self.free_dkeys:
                b.dkey = self.free_dkeys.pop()
            else:
                b.dkey = self.uid("d")
                self.sems[b.dkey] = self.nc.alloc_semaphore("s_" + b.dkey)
                self.dcnt[b.dkey] = 0
            self.dbufs.append(b)
        for rb in rbufs:
            if rb.lw is not None:
                self._wait(q, *rb.lw)
        for wb in wbufs:
            if wb.lw is not None and wb.lw[0] != b.dkey:
                self._wait(q, *wb.lw)
            for key, val in wb.rd.items():
                self._wait(q, key, val)
        ins = self.eng[q].dma_start(out=out, in_=in_)
        self.nops += 1
        self.dcnt[b.dkey] += 16
        c = self.dcnt[b.dkey]
        ins.then_inc(self.sems[b.dkey], 16)
        for rb in rbufs:
            rb.rd[b.dkey] = c
        for wb in wbufs:
            wb.lw = (b.dkey, c)
            wb.rd = {}
        return ins

    def collective(self, kind, groups, in_ap, out_ap, in_buf, out_buf):
        if "cc" not in self.sems:
            self.sems["cc"] = self.nc.alloc_semaphore("s_cc")
            self.dcnt["cc"] = 0
        ins = self.eng["pool"].collective_compute(kind, ALU.bypass, replica_groups=groups,
                                                  ins=[in_ap], outs=[out_ap])
        self.dcnt["cc"] += 1
        ins.then_inc(self.sems["cc"], 1)
        in_buf.rd["cc"] = self.dcnt["cc"]
        out_buf.lw = ("cc", self.dcnt["cc"])
        out_buf.rd = {}

    def barrier(self):
        for e in ("pe", "act", "dve", "pool", "sp"):
            for key in ("pe", "act", "dve", "pool"):
                if key != e and self.cnt[key] > 0:
                    self._wait(e, key, self.cnt[key])
            for key, c in self.dcnt.items():
                if c > 0:
                    self._wait(e, key, c)

    def release_dsems(self):
        for b in self.dbufs:
            if b.dkey is not None:
                self.free_dkeys.append(b.dkey)
                b.dkey = None
        self.dbufs = []


D = 4096
NKC = D // 128
EPS = 1e-6
NEC = 57
EC_FQ, EC_FK, EC_FV, EC_FZ = 0, 4, 8, 12
EC_SX, EC_SB, EC_SC, EC_SZ = 16, 22, 24, 26
EC_GQ, EC_GK, EC_GV, EC_GZ = 32, 38, 44, 50
EC_SM = 56

_OFF = np.cumsum([0, 3072, 8, 1024, 2560, 1536, 24, 4608, 1536, 12, 12])


def local_cols(hh):
    o = _OFF
    c = []
    r = np.arange
    c += list(o[0] + hh * 512 + r(512))
    c += list(o[0] + 1024 + hh * 512 + r(512))
    c += list(o[0] + 2048 + hh * 512 + r(512))
    c += list(o[2] + hh * 512 + r(512))
    c += list(o[3] + hh * 768 + r(768))
    c += list(o[3] + 1536 + hh * 256 + r(256))
    c += list(o[3] + 2048 + hh * 256 + r(256))
    c += list(o[4] + hh * 768 + r(768))
    c += list(o[6] + hh * 768 + r(768))
    c += list(o[6] + 1536 + hh * 768 + r(768))
    c += list(o[6] + 3072 + hh * 768 + r(768))
    c += list(o[7] + hh * 768 + r(768))
    c += list(o[1] + hh * 4 + r(4))
    c += list(o[5] + hh * 12 + r(12))
    c += list(o[8] + hh * 6 + r(6))
    c += list(o[9] + hh * 6 + r(6))
    c += [-1] * (NEC * 128 - len(c))
    return np.array(c, dtype=np.int64)


def prep_win(w_in_l, hh):
    cols = local_cols(hh)
    wz = np.concatenate([w_in_l, np.zeros((D, 1), np.float32)], axis=1)
    wl = wz[:, cols]
    wl = wl.reshape(NKC, 128, NEC, 128).transpose(2, 1, 0, 3)
    return np.ascontiguousarray(wl).reshape(NEC, 128, NKC * 128)


def stage_inproj(kb, S, x_d, win_d, nwb_d, proj_d, C, TT=512, ec_list=None, xload=None):
    if ec_list is None:
        ec_list = list(range(NEC))
    nT = S // TT
    nsub = TT // 128
    with ExitStack() as st:
        xs = [kb.sb(st, [128, D], F32, "xs") for _ in range(2)]
        junk = kb.sb(st, [128, D], BF16, "junk")
        hb = [kb.sb(st, [128, D], BF16, "hb") for _ in range(2)]
        nwb = kb.sb(st, [128, D], F32, "nwb")
        hT = [kb.sb(st, [128, NKC, TT], BF16, "hT") for _ in range(2)]
        wb = [kb.sb(st, [128, NKC, 128], BF16, "wb") for _ in range(3)]
        stg = [kb.sb(st, [128, TT], F32, "stg") for _ in range(3)]
        ss = [kb.sb(st, [128, 1], F32, "ss") for _ in range(2)]
        rs = [kb.sb(st, [128, 1], F32, "rs") for _ in range(2)]
        pst = [kb.ps(st, [128, 512], BF16, "pst") for _ in range(2)]
        psm = [kb.ps(st, [128, 512], F32, "psm") for _ in range(3)]
        ident = C["ident_bf"]
        epsb = C["eps"]

        kb.dma("sp", nwb[:], nwb_d, [nwb], [], nwb)
        nsubs = S // 128

        def load_x(j):
            if xload is not None:
                xload(kb, xs[j % 2], j)
            else:
                kb.dma("sp", xs[j % 2][:], x_d[j * 128:(j + 1) * 128, :], [xs[j % 2]], [], xs[j % 2])

        cnt = {"ev": 0, "pt": 0}

        def norm_tile(tt):
            for js in range(nsub):
                j = tt * nsub + js
                if j + 1 < nsubs:
                    load_x(j + 1)
                x_, h_, s_, r_ = xs[j % 2], hb[j % 2], ss[j % 2], rs[j % 2]
                kb.op("act", lambda e: e.activation(out=junk[:], in_=x_[:], func=AF.Square,
                                                    accum_out=s_[:]),
                      reads=[x_], writes=[junk, s_])
                kb.op("act", lambda e: e.activation(out=r_[:], in_=s_[:], func=AF.Ln,
                                                    scale=1.0 / D, bias=epsb[:]),
                      reads=[s_, epsb], writes=[r_])
                kb.op("act", lambda e: e.activation(out=r_[:], in_=r_[:], func=AF.Exp, scale=-0.5), reads=[r_], writes=[r_])
                kb.op("dve", lambda e: e.scalar_tensor_tensor(out=h_[:], in0=x_[:], scalar=r_[:, 0:1],
                                                              in1=nwb[:], op0=ALU.mult, op1=ALU.mult),
                      reads=[x_, r_, nwb], writes=[h_])
                for g in range(NKC // 4):
                    p_ = pst[cnt["pt"] % 2]
                    cnt["pt"] += 1
                    for q in range(4):
                        kc = g * 4 + q
                        kb.op("pe", lambda e: e.transpose(out=p_[:, q * 128:(q + 1) * 128],
                                                          in_=h_[:, kc * 128:(kc + 1) * 128],
                                                          identity=ident[:]),
                              reads=[h_, ident], writes=[p_], inc=(q == 3))
                    ev = "dve" if cnt["ev"] % 2 == 0 else "act"
                    cnt["ev"] += 1
                    dst = hT[tt % 2][:, g * 4:(g + 1) * 4, js * 128:(js + 1) * 128]
                    src = p_[:].rearrange("p (a b) -> p a b", a=4)
                    if ev == "dve":
                        kb.op("dve", lambda e: e.tensor_copy(out=dst, in_=src), reads=[p_], writes=[hT[tt % 2]])
                    else:
                        kb.op("act", lambda e: e.activation(out=dst, in_=src, func=AF.Copy),
                              reads=[p_], writes=[hT[tt % 2]])

        wq = {"n": 0}
        seq = [(tt, ec) for tt in range(nT) for ec in ec_list]

        def load_w(i):
            if i < len(seq):
                ec = seq[i][1]
                b_ = wb[i % 3]
                kb.dma("pool", b_[:].rearrange("p a b -> p (a b)"), win_d[ec], [b_], [], b_)

        load_x(0)
        norm_tile(0)
        load_w(0)
        load_w(1)
        i = 0
        for tt in range(nT):
            if tt + 1 < nT:
                norm_tile(tt + 1)
            for ec in ec_list:
                load_w(i + 2)
                w_ = wb[i % 3]
                p_ = psm[i % 3]
                s_ = stg[i % 3]
                h_ = hT[tt % 2]
                for kc in range(NKC):
                    kb.op("pe", lambda e: e.matmul(p_[:, :TT], lhsT=w_[:, kc, :], rhs=h_[:, kc, :],
                                                   start=(kc == 0), stop=(kc == NKC - 1)),
                          reads=[w_, h_], writes=[p_], inc=(kc == NKC - 1))
                if i % 2 == 0:
                    kb.op("act", lambda e: e.activation(out=s_[:], in_=p_[:, :TT], func=AF.Copy),
                          reads=[p_], writes=[s_])
                else:
                    kb.op("dve", lambda e: e.tensor_copy(out=s_[:], in_=p_[:, :TT]), reads=[p_], writes=[s_])
                kb.dma("sp", proj_d[ec * 128:(ec + 1) * 128, tt * TT:(tt + 1) * TT], s_[:],
                       [Buf(None, "projreg")], [s_], s_)
                i += 1
        kb.barrier()
        kb.release_dsems()


NEGM = -30000.0
MIXDT = BF16
SMW = 32


def const_arrays():
    c = {}
    r = np.arange(128)
    c["ident_bf"] = np.eye(128, dtype=np.float32)
    c["ident_f"] = np.eye(128, dtype=np.float32)
    c["ones_f"] = np.ones((128, 128), np.float32)
    c["eps"] = np.full((128, 1), EPS, np.float32)
    c["tri_f"] = (r[:, None] <= r[None, :]).astype(np.float32)
    c["tri_bf"] = c["tri_f"].copy()
    c["striT"] = (r[:, None] > r[None, :]).astype(np.float32)
    c["negm_ns"] = np.where(r[:, None] > r[None, :], NEGM, 0.0).astype(np.float32)
    c["negm_s"] = np.where(r[:, None] >= r[None, :], NEGM, 0.0).astype(np.float32)
    c["negm_sT"] = np.where(r[None, :] >= r[:, None], NEGM, 0.0).astype(np.float32)
    c["negm_ns4"] = np.tile(c["negm_ns"], (1, 4))
    c["half64"] = np.broadcast_to((r[:, None] <= 63), (128, 128)).astype(np.float32).copy()
    return c


CONST_DT = {"ident_bf": BF16, "tri_bf": BF16}


def load_consts(kb, st, cd):
    C = {}
    for name, (ap, shape, dt) in cd.items():
        b = kb.sb(st, shape, dt, name)
        q = "pool" if dt == BF16 else "sp"
        kb.dma(q, b[:], ap, [b], [], b)
        C[name] = b
    return C


PC = {"fwq": 0, "fwk": 1, "fwo": 2, "gnw": 3, "snw": 4, "sconvb": 10, "sconv": 20, "gconv": 60}
NPC = 60 + 18 * 4


def prep_params(inp, l, hh):
    o = _OFF
    cols = local_cols(hh)
    pc = np.zeros((128, NPC), np.float32)
    pc[:, PC["fwq"]] = inp["fox_q_norm_w"][l]
    pc[:, PC["fwk"]] = inp["fox_k_norm_w"][l]
    pc[:, PC["fwo"]] = inp["fox_out_norm_w"][l]
    pc[:, PC["gnw"]] = inp["gdn_norm_w"][l]
    for i in range(6):
        pc[:, PC["snw"] + i] = inp["ssd_norm_w"][l][hh * 768 + i * 128: hh * 768 + (i + 1) * 128]
    for i in range(10):
        ch = cols[(EC_SX + i) * 128:(EC_SX + i + 1) * 128] - o[3]
        pc[:, PC["sconvb"] + i] = inp["ssd_conv_b"][l][ch]
        pc[:, PC["sconv"] + 4 * i: PC["sconv"] + 4 * i + 4] = inp["ssd_conv_w"][l][:, ch].T
    for i in range(18):
        ch = cols[(EC_GQ + i) * 128:(EC_GQ + i + 1) * 128] - o[6]
        pc[:, PC["gconv"] + 4 * i: PC["gconv"] + 4 * i + 4] = inp["gdn_conv_w"][l][:, ch].T
    smb = np.zeros(SMW, np.float32)
    sgn = np.ones(SMW, np.float32)
    alog = np.zeros(SMW, np.float32)
    smb[0:4] = inp["fox_b_f"][l][hh * 4: hh * 4 + 4]
    smb[4:16] = inp["ssd_dt_bias"][l][hh * 12: hh * 12 + 12]
    smb[22:28] = inp["gdn_dt_bias"][l][hh * 6: hh * 6 + 6]
    sgn[0:4] = -1.0
    sgn[16:22] = -1.0
    alog[4:16] = inp["ssd_A_log"][l][hh * 12: hh * 12 + 12]
    alog[22:28] = inp["gdn_A_log"][l][hh * 6: hh * 6 + 6]
    dsk = np.repeat(inp["ssd_D"][l][hh * 12: hh * 12 + 12], 64)
    pr = np.concatenate([smb, sgn, alog, dsk]).astype(np.float32)
    pr = np.broadcast_to(pr, (128, pr.size)).copy()
    return pc, pr


NPR = 3 * SMW + 768


def run_interleaved(gens):
    gens = list(gens)
    while gens:
        for g in list(gens):
            try:
                next(g)
            except StopIteration:
                gens.remove(g)


def pnorm(kb, W, src, dst, S, scale, wcol, mul):
    C = W["C"]
    for b0 in range(0, S, 512):
        n = min(512, S - b0)
        i = W["pn_i"]
        W["pn_i"] += 1
        sq, rt, pp = W["sq"][i % 2], W["rt"][i % 2], W["pn"][i % 2]
        kb.op("act", lambda e: e.activation(out=sq[:, :n], in_=src[:, b0:b0 + n], func=AF.Square),
              reads=[src], writes=[sq])
        kb.op("pe", lambda e: e.matmul(pp[:, :n], lhsT=C["ones_f"][:], rhs=sq[:, :n], start=True, stop=True),
              reads=[C["ones_f"], sq], writes=[pp])
        kb.op("act", lambda e: e.activation(out=rt[:, :n], in_=pp[:, :n], func=AF.Ln, scale=scale,
                                            bias=C["eps"][:]), reads=[pp, C["eps"]], writes=[rt])
        kb.op("act", lambda e: e.activation(out=rt[:, :n], in_=rt[:, :n], func=AF.Exp, scale=-0.5), reads=[rt], writes=[rt])
        if wcol is None:
            kb.op("dve", lambda e: e.scalar_tensor_tensor(out=dst[:, b0:b0 + n], in0=src[:, b0:b0 + n], scalar=mul,
                                                          in1=rt[:, :n], op0=ALU.mult, op1=ALU.mult),
                  reads=[src, rt], writes=[dst])
        else:
            kb.op("dve", lambda e: e.scalar_tensor_tensor(out=dst[:, b0:b0 + n], in0=src[:, b0:b0 + n], scalar=wcol,
                                                          in1=rt[:, :n], op0=ALU.mult, op1=ALU.mult),
                  reads=[src, rt, W["pc"]], writes=[dst])


def conv_silu(kb, W, proj_d, ec, wofs, bofs, dst, S, fp32dst):
    raw = W["craw"]
    acc = dst if fp32dst else W["cacc"]
    pc = W["pc"]
    kb.dma("sp", raw[:, 3:3 + S], proj_d[ec * 128:(ec + 1) * 128, 0:S], [raw], [], raw)
    kb.op("act", lambda e: e.activation(out=acc[:, :S], in_=raw[:, 0:S], func=AF.Copy, scale=pc[:, wofs:wofs + 1]),
          reads=[raw, pc], writes=[acc])
    for k in range(1, 4):
        kb.op("dve", lambda e: e.scalar_tensor_tensor(out=acc[:, :S], in0=raw[:, k:k + S],
                                                      scalar=pc[:, wofs + k:wofs + k + 1], in1=acc[:, :S],
                                                      op0=ALU.mult, op1=ALU.add), reads=[raw, pc, acc], writes=[acc])
    if bofs is None:
        kb.op("act", lambda e: e.activation(out=dst[:, :S], in_=acc[:, :S], func=AF.Silu), reads=[acc], writes=[dst])
    else:
        kb.op("act", lambda e: e.activation(out=dst[:, :S], in_=acc[:, :S], func=AF.Silu, bias=pc[:, bofs:bofs + 1]),
              reads=[acc, pc], writes=[dst])


def alloc_conv(kb, st, W, S):
    W["craw"] = kb.sb(st, [128, S + 3], F32, "craw")
    W["cacc"] = kb.sb(st, [128, S], F32, "cacc")
    kb.op("dve", lambda e: e.memset(W["craw"][:, 0:3], 0.0), reads=[], writes=[W["craw"]])


def alloc_common(kb, st, S, C, pc_d, pr_d):
    W = {"C": C, "pn_i": 0, "cv_i": 0}
    W["pc"] = kb.sb(st, [128, NPC], F32, "pc")
    W["pr"] = kb.sb(st, [128, NPR], F32, "pr")
    kb.dma("sp", W["pc"][:], pc_d, [W["pc"]], [], W["pc"])
    kb.dma("sp", W["pr"][:], pr_d, [W["pr"]], [], W["pr"])
    W["sq"] = [kb.sb(st, [128, 512], F32, "sq") for _ in range(2)]
    W["rt"] = [kb.sb(st, [128, 512], F32, "rt") for _ in range(2)]
    return W


def stage_small(kb, st, W, S, proj_d):
    C = W["C"]
    nblk = S // 128
    T = {}
    T["rate"] = kb.sb(st, [128, nblk, SMW], F32, "rate")
    T["sp"] = kb.sb(st, [128, nblk, SMW], F32, "sp")
    T["beta"] = kb.sb(st, [128, nblk, SMW], F32, "beta")
    with ExitStack() as s2:
        smr = kb.sb(s2, [SMW, S], F32, "smr")
        sm = kb.sb(s2, [128, nblk, SMW], F32, "sm")
        ax = kb.sb(s2, [128, nblk, SMW], F32, "ax")
        nega = kb.sb(s2, [128, SMW], F32, "nega")
        pp = W["pn"][0]
        kb.dma("sp", smr[:], proj_d[EC_SM * 128:EC_SM * 128 + SMW, 0:S], [smr], [], smr)
        for b0 in range(0, nblk, 16):
            nb = min(16, nblk - b0)
            for j in range(nb):
                blk = b0 + j
                kb.op("pe", lambda e: e.transpose(out=pp[:, j * SMW:(j + 1) * SMW],
                                                  in_=smr[0:SMW, blk * 128:(blk + 1) * 128],
                                                  identity=C["ident_f"][0:SMW, 0:SMW]),
                      reads=[smr, C["ident_f"]], writes=[pp], inc=(j == nb - 1))
            kb.op("dve", lambda e: e.tensor_copy(out=sm[:, b0:b0 + nb, :].rearrange("p a b -> p (a b)"),
                                                 in_=pp[:, :nb * SMW]), reads=[pp], writes=[sm])
        pr = W["pr"]

        def bc(off):
            return pr[:, off:off + SMW].unsqueeze(1).broadcast_to([128, nblk, SMW])
        kb.op("dve", lambda e: e.tensor_tensor(out=sm[:], in0=sm[:], in1=bc(0), op=ALU.add), reads=[sm, pr], writes=[sm])
        kb.op("dve", lambda e: e.tensor_tensor(out=sm[:], in0=sm[:], in1=bc(SMW), op=ALU.mult), reads=[sm, pr], writes=[sm])
        kb.op("act", lambda e: e.activation(out=ax[:], in_=sm[:], func=AF.Abs), reads=[sm], writes=[ax])
        kb.op("act", lambda e: e.activation(out=ax[:], in_=ax[:], func=AF.Exp, scale=-1.0), reads=[ax], writes=[ax])
        kb.op("act", lambda e: e.activation(out=ax[:], in_=ax[:], func=AF.Ln, bias=1.0), reads=[ax], writes=[ax])
        kb.op("dve", lambda e: e.scalar_tensor_tensor(out=T["sp"][:], in0=sm[:], scalar=0.0, in1=ax[:],
                                                      op0=ALU.max, op1=ALU.add), reads=[sm, ax], writes=[T["sp"]])
        kb.op("act", lambda e: e.activation(out=nega[:], in_=pr[:, 2 * SMW:3 * SMW], func=AF.Exp), reads=[pr], writes=[nega])
        kb.op("dve", lambda e: e.scalar_tensor_tensor(out=T["rate"][:], in0=T["sp"][:], scalar=-1.0,
                                                      in1=nega[:].unsqueeze(1).broadcast_to([128, nblk, SMW]),
                                                      op0=ALU.mult, op1=ALU.mult), reads=[T["sp"], nega], writes=[T["rate"]])
        kb.op("act", lambda e: e.activation(out=T["beta"][:], in_=T["rate"][:], func=AF.Exp), reads=[T["rate"]],
              writes=[T["beta"]])
        kb.barrier()
    return T


def stage_fox(kb, W, T, S, proj_d, mix_d):
    C = W["C"]
    pc = W["pc"]
    nblk = S // 128
    with ExitStack() as st:
        raw = [kb.sb(st, [128, S], F32, "fraw") for _ in range(2)]
        Fcol = kb.sb(st, [128, nblk, 4], F32, "Fcol")
        pre = kb.sb(st, [128, nblk + 1, 4], F32, "pre")
        bsum = kb.sb(st, [128, nblk, 4], F32, "bsum")
        Fmid = kb.sb(st, [128, nblk, 4], F32, "Fmid")
        lfc = kb.sb(st, [128, nblk, 4], F32, "lfc")
        ptr = W["pn"][1]
        slots = []
        for sl in range(2):
            X = {}
            X["qn"] = kb.sb(st, [128, S], BF16, "qn")
            X["kn"] = kb.sb(st, [128, S], BF16, "kn")
            X["vtok"] = kb.sb(st, [128, nblk, 132], BF16, "vtok")
            X["sz"] = kb.sb(st, [128, S], F32, "sz")
            X["mixo"] = kb.sb(st, [128, S], MIXDT, "mixo")
            X["pt"] = [kb.sb(st, [128, 4, 128], BF16, "pt") for _ in range(3)]
            X["bias"] = [kb.sb(st, [128, nblk], F32, "fbias") for _ in range(2)]
            X["osb"] = [kb.sb(st, [128, 128], F32, "osb") for _ in range(2)]
            X["col"] = [kb.sb(st, [128, 4], F32, "fcol") for _ in range(2)]
            X["junk"] = kb.sb(st, [128, 128], F32, "fjunk")
            X["pss"] = [kb.ps(st, [128, 512], F32, "pss") for _ in range(2)]
            X["po"] = kb.ps(st, [128, 512], F32, "po")
            kb.op("dve", lambda e: e.memset(X["vtok"][:, :, 128:129], 1.0), reads=[], writes=[X["vtok"]])
            slots.append(X)
        logf = T["rate"]

        lf = logf[:, :, 0:4]
        pp = W["pn"][0]
        pv = pp[:, 0:nblk * 4].rearrange("p (a b) -> p a b", b=4)
        kb.op("dve", lambda e: e.tensor_copy(out=lfc[:], in_=lf), reads=[logf], writes=[lfc])
        lff = lfc[:].rearrange("p a b -> p (a b)")
        kb.op("pe", lambda e: e.matmul(pp[:, 0:nblk * 4], lhsT=C["tri_f"][:], rhs=lff, start=True, stop=True),
              reads=[C["tri_f"], lfc], writes=[pp])
        kb.op("dve", lambda e: e.tensor_copy(out=Fcol[:], in_=pv), reads=[pp], writes=[Fcol])
        kb.op("pe", lambda e: e.matmul(pp[:, 0:nblk * 4], lhsT=C["ones_f"][:], rhs=lff, start=True, stop=True),
              reads=[C["ones_f"], lfc], writes=[pp])
        kb.op("dve", lambda e: e.tensor_copy(out=bsum[:], in_=pv), reads=[pp], writes=[bsum])
        kb.op("pe", lambda e: e.matmul(pp[:, 0:nblk * 4], lhsT=C["half64"][:], rhs=lff, start=True, stop=True),
              reads=[C["half64"], lfc], writes=[pp])
        kb.op("dve", lambda e: e.tensor_copy(out=Fmid[:], in_=pv), reads=[pp], writes=[Fmid])
        kb.op("dve", lambda e: e.memset(pre[:, 0, :], 0.0), reads=[], writes=[pre])
        for j in range(nblk):
            kb.op("dve", lambda e: e.tensor_tensor(out=pre[:, j + 1, :], in0=pre[:, j, :], in1=bsum[:, j, :], op=ALU.add),
                  reads=[pre, bsum], writes=[pre])
        kb.op("dve", lambda e: e.tensor_tensor(out=Fcol[:], in0=Fcol[:], in1=pre[:, 0:nblk, :], op=ALU.add),
              reads=[Fcol, pre], writes=[Fcol])
        kb.op("dve", lambda e: e.tensor_tensor(out=Fmid[:], in0=Fmid[:], in1=pre[:, 0:nblk, :], op=ALU.add),
              reads=[Fmid, pre], writes=[Fmid])

        sc = 128.0 ** -0.5

        def setup(X, h):
            qn, kn, vtok, sz = X["qn"], X["kn"], X["vtok"], X["sz"]
            kb.dma("sp", raw[0][:], proj_d[(EC_FQ + h) * 128:(EC_FQ + h + 1) * 128, 0:S], [raw[0]], [], raw[0])
            kb.dma("sp", raw[1][:], proj_d[(EC_FK + h) * 128:(EC_FK + h + 1) * 128, 0:S], [raw[1]], [], raw[1])
            pnorm(kb, W, raw[0], qn, S, 1.0 / 128, pc[:, PC["fwq"]:PC["fwq"] + 1], 1.0)
            pnorm(kb, W, raw[1], kn, S, 1.0 / 128, pc[:, PC["fwk"]:PC["fwk"] + 1], 1.0)
            kb.dma("sp", raw[0][:], proj_d[(EC_FV + h) * 128:(EC_FV + h + 1) * 128, 0:S], [raw[0]], [], raw[0])
            kb.dma("sp", raw[1][:], proj_d[(EC_FZ + h) * 128:(EC_FZ + h + 1) * 128, 0:S], [raw[1]], [], raw[1])
            for b0 in range(0, nblk, 4):
                nv = min(4, nblk - b0)
                for j in range(nv):
                    kb.op("pe", lambda e: e.transpose(out=ptr[:, j * 128:(j + 1) * 128],
                                                      in_=raw[0][:, (b0 + j) * 128:(b0 + j + 1) * 128],
                                                      identity=C["ident_f"][:]),
                          reads=[raw[0], C["ident_f"]], writes=[ptr], inc=(j == nv - 1))
                kb.op("act", lambda e: e.activation(out=vtok[:, b0:b0 + nv, 0:128],
                                                    in_=ptr[:, 0:nv * 128].rearrange("p (a b) -> p a b", a=nv), func=AF.Copy),
                      reads=[ptr], writes=[vtok])
            kb.op("act", lambda e: e.activation(out=sz[:], in_=raw[1][:], func=AF.Silu), reads=[raw[1]], writes=[sz])

        def qtiles(X, h):
            qn, kn, vtok, sz, mixo = X["qn"], X["kn"], X["vtok"], X["sz"], X["mixo"]
            o_ = X["po"]
            batches = []
            for i in range(nblk):
                js = list(range(0, i + 1, 4))
                for bi_, j0 in enumerate(js):
                    batches.append((i, j0, min(4, i + 1 - j0), bi_ == len(js) - 1))

            def emit_bias(i):
                bi = X["bias"][i % 2]
                kb.op("dve", lambda e: e.tensor_scalar(out=bi[:], in0=Fcol[:, :, h], scalar1=-1.0,
                                                       scalar2=Fmid[:, i, h:h + 1], op0=ALU.mult, op1=ALU.add),
                      reads=[Fcol, Fmid], writes=[bi])

            def emit_qk(b):
                i, j0, nj, _ = batches[b]
                s_ = X["pss"][b % 2]
                for jj in range(nj):
                    j = j0 + jj
                    kb.op("pe", lambda e: e.matmul(s_[:, jj * 128:(jj + 1) * 128], lhsT=kn[:, j * 128:(j + 1) * 128],
                                                   rhs=qn[:, i * 128:(i + 1) * 128], start=True, stop=True),
                          reads=[kn, qn], writes=[s_], inc=(jj == nj - 1))

            def emit_exp(b):
                i, j0, nj, _ = batches[b]
                s_ = X["pss"][b % 2]
                p_ = X["pt"][b % 3]
                bi = X["bias"][i % 2]
                for jj in range(nj):
                    j = j0 + jj
                    kb.op("act", lambda e: e.activation(out=p_[:, jj, :], in_=s_[:, jj * 128:(jj + 1) * 128],
                                                        func=AF.Exp, scale=sc, bias=bi[:, j:j + 1]),
                          reads=[s_, bi], writes=[p_])
                    if j == i:
                        kb.op("dve", lambda e: e.tensor_tensor(out=p_[:, jj, :], in0=p_[:, jj, :],
                                                               in1=C["tri_bf"][:], op=ALU.mult),
                              reads=[p_, C["tri_bf"]], writes=[p_])

            def emit_pv(b):
                i, j0, nj, _ = batches[b]
                p_ = X["pt"][b % 3]
                for jj in range(nj):
                    j = j0 + jj
                    kb.op("pe", lambda e: e.matmul(o_[:, 0:129], lhsT=p_[:, jj, :], rhs=vtok[:, j, 0:129],
                                                   start=(j == 0), stop=(j == i)),
                          reads=[p_, vtok], writes=[o_], inc=(jj == nj - 1))

            def emit_fin(i):
                c_ = X["col"][i % 2]
                ob = X["osb"][i % 2]
                kb.op("dve", lambda e: e.reciprocal(out=c_[:, 0:1], in_=o_[:, 128:129]), reads=[o_], writes=[c_])
                kb.op("dve", lambda e: e.tensor_scalar(out=ob[:], in0=o_[:, 0:128], scalar1=c_[:, 0:1], scalar2=None,
                                                       op0=ALU.mult), reads=[o_, c_], writes=[ob])
                yield
                kb.op("act", lambda e: e.activation(out=X["junk"][:], in_=ob[:], func=AF.Square, accum_out=c_[:, 1:2]),
                      reads=[ob], writes=[X["junk"], c_])
                kb.op("act", lambda e: e.activation(out=c_[:, 2:3], in_=c_[:, 1:2], func=AF.Ln, scale=1.0 / 128,
                                                    bias=C["eps"][:]), reads=[c_, C["eps"]], writes=[c_])
                kb.op("act", lambda e: e.activation(out=c_[:, 3:4], in_=c_[:, 2:3], func=AF.Exp, scale=-0.5), reads=[c_], writes=[c_])
                yield
                kb.op("dve", lambda e: e.tensor_scalar(out=ob[:], in0=ob[:], scalar1=c_[:, 3:4], scalar2=None,
                                                       op0=ALU.mult), reads=[ob, c_], writes=[ob])
                yield
                kb.op("pe", lambda e: e.transpose(out=ptr[:, 0:128], in_=ob[:], identity=C["ident_f"][:]),
                      reads=[ob, C["ident_f"]], writes=[ptr])
                kb.op("dve", lambda e: e.scalar_tensor_tensor(out=mixo[:, i * 128:(i + 1) * 128], in0=ptr[:, 0:128],
                                                              scalar=pc[:, PC["fwo"]:PC["fwo"] + 1],
                                                              in1=sz[:, i * 128:(i + 1) * 128],
                                                              op0=ALU.mult, op1=ALU.mult),
                      reads=[ptr, pc, sz], writes=[mixo])
                yield

            emit_bias(0)
            emit_qk(0)
            yield
            for b in range(len(batches)):
                i, j0, nj, last = batches[b]
                if b + 1 < len(batches):
                    i2, j02, _, _ = batches[b + 1]
                    if j02 == 0:
                        emit_bias(i2)
                    emit_qk(b + 1)
                    yield
                emit_exp(b)
                yield
                emit_pv(b)
                yield
                if last:
                    yield from emit_fin(i)
            kb.dma("sp", mix_d[:, h * 128:(h + 1) * 128, :].rearrange("n p t -> p n t"),
                   mixo[:].rearrange("p (n t) -> p n t", t=mix_th(S)), [Buf(None, "m")], [mixo], mixo)

        for hp in range(2):
            for sl in range(2):
                setup(slots[sl], 2 * hp + sl)
            run_interleaved([qtiles(slots[sl], 2 * hp + sl) for sl in range(2)])
        kb.barrier()
        kb.release_dsems()


def stage_ssd(kb, W, T, S, proj_d, mix_d):
    C = W["C"]
    pc = W["pc"]
    pr = W["pr"]
    nblk = S // 128
    rate, spt = T["rate"], T["sp"]
    with ExitStack() as st:
        W["craw"] = kb.sb(st, [128, S + 3], F32, "craw")
        kb.op("dve", lambda e: e.memset(W["craw"][:, 0:3], 0.0), reads=[], writes=[W["craw"]])
        xT = [kb.sb(st, [128, S], F32, "sxT") for _ in range(3)]
        W["cacc"] = xT[0]
        zsb = [kb.sb(st, [128, 3, 128], F32, "szs") for _ in range(2)]
        BTb = kb.sb(st, [128, S], BF16, "BTb")
        CTb = kb.sb(st, [128, S], BF16, "CTb")
        hst = kb.sb(st, [128, 384], F32, "hst")
        hbf = kb.sb(st, [128, 384], BF16, "hbf")
        R2 = range(2)
        xtok = [kb.sb(st, [128, 384], F32, "xtok") for _ in R2]
        xdt = [kb.sb(st, [128, 384], BF16, "xdt") for _ in R2]
        Btok = [kb.sb(st, [128, 128], BF16, "Btok") for _ in R2]
        at = [kb.sb(st, [128, 6, 128], F32, "at") for _ in R2]
        LT = [kb.sb(st, [128, 6, 128], F32, "LT") for _ in R2]
        MT = [kb.sb(st, [128, 6, 128], BF16, "MT") for _ in R2]
        gts = [kb.sb(st, [128, 128], F32, "gts") for _ in R2]
        ecol = [kb.sb(st, [128, 12], F32, "ecol") for _ in R2]
        t1 = [kb.sb(st, [128, 384], F32, "t1") for _ in R2]
        t2 = [kb.sb(st, [128, 384], F32, "t2") for _ in R2]
        ytok = [kb.sb(st, [128, 384], F32, "ytok") for _ in R2]
        xw = [kb.sb(st, [128, 384], BF16, "xw") for _ in R2]
        gsb = [kb.sb(st, [128, 3, 128], F32, "gsb") for _ in R2]
        sqb = [kb.sb(st, [128, 384], F32, "ssq") for _ in R2]
        rtt = [kb.sb(st, [128, 128], F32, "rtt") for _ in R2]
        osb = [kb.sb(st, [128, 3, 128], MIXDT, "sosb") for _ in R2]
        pn0, pn1 = W["pn"]
        pD0 = kb.ps(st, [128, 512], F32, "pD0")
        pD1 = kb.ps(st, [128, 512], F32, "pD1")
        pY = kb.ps(st, [128, 512], F32, "pY")
        pYo = kb.ps(st, [128, 512], F32, "pYo")
        pS = kb.ps(st, [128, 512], F32, "pS")
        pT = kb.ps(st, [128, 512], F32, "pT")

        for g in range(2):
            conv_silu(kb, W, proj_d, EC_SB + g, PC["sconv"] + 4 * (6 + g), PC["sconvb"] + 6 + g, BTb, S, False)
            conv_silu(kb, W, proj_d, EC_SC + g, PC["sconv"] + 4 * (8 + g), PC["sconvb"] + 8 + g, CTb, S, False)
            zreg = Buf(None, "zreg")
            for k3 in range(3):
                i = g * 3 + k3
                conv_silu(kb, W, proj_d, EC_SX + i, PC["sconv"] + 4 * i, PC["sconvb"] + i, xT[k3], S, True)
                zrows = proj_d[(EC_SZ + i) * 128:(EC_SZ + i + 1) * 128, 0:S]
                kb.dma("sp", W["craw"][:, 3:3 + S], zrows, [W["craw"]], [], W["craw"])
                kb.op("act", lambda e: e.activation(out=W["craw"][:, 3:3 + S], in_=W["craw"][:, 3:3 + S], func=AF.Silu),
                      reads=[W["craw"]], writes=[W["craw"]])
                kb.dma("sp", zrows, W["craw"][:, 3:3 + S], [zreg], [W["craw"]], W["craw"])
            kb.op("dve", lambda e: e.memset(hst[:], 0.0), reads=[], writes=[hst])
            kb.op("dve", lambda e: e.memset(hbf[:], 0.0), reads=[], writes=[hbf])
            c0 = 4 + g * 6
            def Pgen(c):
                    u = c % 2
                    blk = slice(c * 128, (c + 1) * 128)
                    a6 = rate[:, c, c0:c0 + 6]
                    zr0 = (EC_SZ + g * 3) * 128
                    kb.dma("sp", zsb[u][:], proj_d[zr0:zr0 + 384, blk].rearrange("(k p) l -> p k l", p=128),
                           [zsb[u]], [zreg], zsb[u])
                    dt6 = spt[:, c, c0:c0 + 6]
                    for k3 in range(3):
                        kb.op("pe", lambda e: e.transpose(out=pT[:, k3 * 128:(k3 + 1) * 128], in_=xT[k3][:, blk],
                                                          identity=C["ident_f"][:]),
                              reads=[xT[k3], C["ident_f"]], writes=[pT], inc=(k3 == 2))
                    kb.op("act", lambda e: e.activation(out=xtok[u][:], in_=pT[:, 0:384], func=AF.Copy),
                          reads=[pT], writes=[xtok[u]])
                    kb.op("dve", lambda e: e.tensor_tensor(out=xdt[u][:].rearrange("p (h d) -> p h d", h=6),
                                                           in0=xtok[u][:].rearrange("p (h d) -> p h d", h=6),
                                                           in1=dt6.unsqueeze(2).broadcast_to([128, 6, 64]), op=ALU.mult),
                          reads=[xtok[u], spt], writes=[xdt[u]])
                    kb.op("pe", lambda e: e.matmul(pn1[:, 128:256], lhsT=BTb[:, blk], rhs=C["ident_bf"][:], start=True, stop=True),
                          reads=[BTb, C["ident_bf"]], writes=[pn1])
                    kb.op("act", lambda e: e.activation(out=Btok[u][:], in_=pn1[:, 128:256], func=AF.Copy),
                          reads=[pn1], writes=[Btok[u]])
                    yield
                    kb.op("pool", lambda e: e.tensor_tensor(out=at[u][:], in0=C["tri_f"][:].unsqueeze(1).broadcast_to([128, 6, 128]),
                                                           in1=a6.unsqueeze(2).broadcast_to([128, 6, 128]), op=ALU.mult),
                          reads=[C["tri_f"], rate], writes=[at[u]])
                    kb.op("pe", lambda e: e.matmul(pD0[:], lhsT=C["striT"][:], rhs=at[u][:, 0:4, :].rearrange("p a b -> p (a b)"),
                                                   start=True, stop=False), reads=[C["striT"], at[u]], writes=[pD0], inc=False)
                    kb.op("pe", lambda e: e.matmul(pD0[:], lhsT=C["ident_f"][:], rhs=C["negm_ns4"][:], start=False, stop=True),
                          reads=[C["ident_f"], C["negm_ns4"]], writes=[pD0])
                    kb.op("pe", lambda e: e.matmul(pD1[:, 0:256], lhsT=C["striT"][:], rhs=at[u][:, 4:6, :].rearrange("p a b -> p (a b)"),
                                                   start=True, stop=False), reads=[C["striT"], at[u]], writes=[pD1], inc=False)
                    kb.op("pe", lambda e: e.matmul(pD1[:, 0:256], lhsT=C["ident_f"][:], rhs=C["negm_ns4"][:, 0:256], start=False, stop=True),
                          reads=[C["ident_f"], C["negm_ns4"]], writes=[pD1])
                    kb.op("act", lambda e: e.activation(out=LT[u][:, 0:4, :].rearrange("p a b -> p (a b)"), in_=pD0[:], func=AF.Exp),
                          reads=[pD0], writes=[LT[u]])
                    kb.op("act", lambda e: e.activation(out=LT[u][:, 4:6, :].rearrange("p a b -> p (a b)"), in_=pD1[:, 0:256], func=AF.Exp),
                          reads=[pD1], writes=[LT[u]])
                    yield
                    kb.op("pe", lambda e: e.matmul(pn1[:, 0:128], lhsT=BTb[:, blk], rhs=CTb[:, blk], start=True, stop=True),
                          reads=[BTb, CTb], writes=[pn1])
                    kb.op("act", lambda e: e.activation(out=gts[u][:], in_=pn1[:, 0:128], func=AF.Copy), reads=[pn1], writes=[gts[u]])
                    kb.op("pool", lambda e: e.tensor_tensor(out=MT[u][:], in0=LT[u][:],
                                                           in1=gts[u][:].unsqueeze(1).broadcast_to([128, 6, 128]), op=ALU.mult),
                          reads=[LT[u], gts[u]], writes=[MT[u]])
                    yield
                    kb.op("pe", lambda e: e.matmul(pn0[:, 0:6], lhsT=C["tri_f"][:], rhs=a6, start=True, stop=True),
                          reads=[C["tri_f"], rate], writes=[pn0], inc=False)
                    kb.op("pe", lambda e: e.matmul(pn0[:, 6:12], lhsT=C["ones_f"][:], rhs=a6, start=True, stop=True),
                          reads=[C["ones_f"], rate], writes=[pn0])
                    kb.op("act", lambda e: e.activation(out=ecol[u][:], in_=pn0[:, 0:12], func=AF.Exp), reads=[pn0], writes=[ecol[u]])

                    yield

            def Dgen(c):
                    u = c % 2
                    blk = slice(c * 128, (c + 1) * 128)
                    a6 = rate[:, c, c0:c0 + 6]
                    for h in range(6):
                        kb.op("pe", lambda e: e.matmul(pY[:, h * 64:(h + 1) * 64], lhsT=MT[u][:, h, :],
                                                       rhs=xdt[u][:, h * 64:(h + 1) * 64], start=True, stop=True),
                              reads=[MT[u], xdt[u]], writes=[pY], inc=(h == 5))
                    kb.op("pe", lambda e: e.matmul(pYo[:, 0:384], lhsT=CTb[:, blk], rhs=hbf[:], start=True, stop=True),
                          reads=[CTb, hbf], writes=[pYo])
                    kb.op("dve", lambda e: e.tensor_tensor(out=t1[u][:].rearrange("p (h d) -> p h d", h=6),
                                                           in0=pYo[:, 0:384].rearrange("p (h d) -> p h d", h=6),
                                                           in1=ecol[u][:, 0:6].unsqueeze(2).broadcast_to([128, 6, 64]), op=ALU.mult),
                          reads=[pYo, ecol[u]], writes=[t1[u]])
                    kb.op("dve", lambda e: e.tensor_tensor(out=ytok[u][:], in0=pY[:, 0:384], in1=t1[u][:], op=ALU.add),
                          reads=[pY, t1[u]], writes=[ytok[u]])
                    kb.op("pool", lambda e: e.tensor_tensor(out=t2[u][:], in0=xtok[u][:],
                                                           in1=pr[:, 3 * SMW + g * 384:3 * SMW + (g + 1) * 384], op=ALU.mult),
                          reads=[xtok[u], pr], writes=[t2[u]])
                    kb.op("dve", lambda e: e.tensor_tensor(out=ytok[u][:], in0=ytok[u][:], in1=t2[u][:], op=ALU.add),
                          reads=[ytok[u], t2[u]], writes=[ytok[u]])
                    yield
                    kb.op("dve", lambda e: e.tensor_tensor(out=xw[u][:].rearrange("p (h d) -> p h d", h=6),
                                                           in0=xdt[u][:].rearrange("p (h d) -> p h d", h=6),
                                                           in1=LT[u][:, :, 127:128].broadcast_to([128, 6, 64]), op=ALU.mult),
                          reads=[xdt[u], LT[u]], writes=[xw[u]])
                    kb.op("pe", lambda e: e.matmul(pS[:, 0:384], lhsT=Btok[u][:], rhs=xw[u][:], start=True, stop=True),
                          reads=[Btok[u], xw[u]], writes=[pS])
                    kb.op("dve", lambda e: e.tensor_tensor(out=hst[:].rearrange("p (h d) -> p h d", h=6),
                                                           in0=hst[:].rearrange("p (h d) -> p h d", h=6),
                                                           in1=ecol[u][:, 6:12].unsqueeze(2).broadcast_to([128, 6, 64]), op=ALU.mult),
                          reads=[hst, ecol[u]], writes=[hst])
                    kb.op("dve", lambda e: e.tensor_tensor(out=hst[:], in0=hst[:], in1=pS[:, 0:384], op=ALU.add),
                          reads=[hst, pS], writes=[hst])
                    kb.op("act", lambda e: e.activation(out=hbf[:], in_=hst[:], func=AF.Copy), reads=[hst], writes=[hbf])
                    yield
                    for k3 in range(3):
                        kb.op("pe", lambda e: e.transpose(out=pT[:, k3 * 128:(k3 + 1) * 128], in_=ytok[u][:, k3 * 128:(k3 + 1) * 128],
                                                          identity=C["ident_f"][:]),
                              reads=[ytok[u], C["ident_f"]], writes=[pT], inc=(k3 == 2))
                    for k3 in range(3):
                        kb.op("dve", lambda e: e.tensor_tensor(out=gsb[u][:, k3, :], in0=pT[:, k3 * 128:(k3 + 1) * 128],
                                                               in1=zsb[u][:, k3, :], op=ALU.mult),
                              reads=[pT, zsb[u]], writes=[gsb[u]])
                    yield
                    kb.op("act", lambda e: e.activation(out=sqb[u][:], in_=gsb[u][:].rearrange("p a b -> p (a b)"), func=AF.Square),
                          reads=[gsb[u]], writes=[sqb[u]])
                    for k3 in range(3):
                        kb.op("pe", lambda e: e.matmul(pn0[:, 128:256], lhsT=C["ones_f"][:], rhs=sqb[u][:, k3 * 128:(k3 + 1) * 128],
                                                       start=(k3 == 0), stop=(k3 == 2)),
                              reads=[C["ones_f"], sqb[u]], writes=[pn0], inc=(k3 == 2))
                    kb.op("act", lambda e: e.activation(out=rtt[u][:], in_=pn0[:, 128:256], func=AF.Ln, scale=1.0 / 384,
                                                        bias=C["eps"][:]), reads=[pn0, C["eps"]], writes=[rtt[u]])
                    kb.op("act", lambda e: e.activation(out=rtt[u][:], in_=rtt[u][:], func=AF.Exp, scale=-0.5), reads=[rtt[u]], writes=[rtt[u]])
                    for k3 in range(3):
                        wc = PC["snw"] + g * 3 + k3
                        kb.op("dve", lambda e: e.scalar_tensor_tensor(out=osb[u][:, k3, :], in0=gsb[u][:, k3, :],
                                                                      scalar=pc[:, wc:wc + 1], in1=rtt[u][:],
                                                                      op0=ALU.mult, op1=ALU.mult),
                              reads=[gsb[u], pc, rtt[u]], writes=[osb[u]])
                    r0 = 512 + g * 384
                    mth, moff = (c * 128) // mix_th(S), (c * 128) % mix_th(S)
                    kb.dma("sp", mix_d[mth, r0:r0 + 384, moff:moff + 128].rearrange("(k p) l -> p k l", p=128), osb[u][:],
                           [Buf(None, "m")], [osb[u]], osb[u])

                    yield

            run_interleaved([Pgen(0)])
            for c in range(nblk):
                gl = [Dgen(c)]
                if c + 1 < nblk:
                    gl.insert(0, Pgen(c + 1))
                run_interleaved(gl)
        kb.barrier()
        kb.release_dsems()


def stage_gdn(kb, W, T, S, proj_d, mix_d):
    C = W["C"]
    pc = W["pc"]
    nblk = S // 128
    rate, betaT = T["rate"], T["beta"]
    idf = C["ident_f"]
    with ExitStack() as st:
        W["craw"] = kb.sb(st, [128, S + 3], F32, "craw")
        kb.op("dve", lambda e: e.memset(W["craw"][:, 0:3], 0.0), reads=[], writes=[W["craw"]])
        tmp = kb.sb(st, [128, S], F32, "gtmp")
        W["cacc"] = tmp
        vTb = kb.sb(st, [128, S], BF16, "gvT")
        g6 = kb.sb(st, [128, nblk, 6], F32, "g6")
        gc = kb.sb(st, [128, nblk, 6], F32, "gc")
        tot = kb.sb(st, [128, nblk, 6], F32, "tot")
        kbs = kb.sb(st, [128, nblk, 6], F32, "kbs")
        etl = kb.sb(st, [128, nblk, 6], F32, "etl")
        els = kb.sb(st, [128, nblk, 6], F32, "els")
        banks = [kb.ps(st, [128, 512], F32, "gbank") for _ in range(8)]
        R2 = range(2)
        slots = []
        for sl in range(2):
            X = {}
            X["qT"] = kb.sb(st, [128, S], BF16, "gqT")
            X["kTb"] = kb.sb(st, [128, S], BF16, "gkT")
            X["mixo"] = kb.sb(st, [128, S], MIXDT, "gmixo")
            X["Kbg"] = kb.sb(st, [128, nblk, 128], BF16, "Kbg")
            X["Ktl"] = kb.sb(st, [128, nblk, 128], BF16, "Ktl")
            X["Vb"] = kb.sb(st, [128, nblk, 128], BF16, "Vb")
            X["Sst"] = kb.sb(st, [128, 128], F32, "Sst")
            X["Sbf"] = kb.sb(st, [128, 128], BF16, "Sbf")
            X["at"] = [kb.sb(st, [128, 128], F32, "gat") for _ in R2]
            X["dl"] = [kb.sb(st, [128, 128], F32, "gdl") for _ in R2]
            X["E124"] = [kb.sb(st, [128, 384], F32, "E124") for _ in R2]
            X["E3"] = [kb.sb(st, [128, 128], F32, "E3") for _ in R2]
            X["XX"] = [kb.sb(st, [128, 256], F32, "XX") for _ in range(3)]
            X["Pm"] = [kb.sb(st, [128, 128], F32, "Pm") for _ in range(3)]
            X["TTb"] = [kb.sb(st, [128, 128], BF16, "TTb") for _ in R2]
            X["qkT"] = [kb.sb(st, [128, 128], BF16, "qkT") for _ in R2]
            X["qdT"] = [kb.sb(st, [128, 128], BF16, "qdT") for _ in R2]
            X["wTb"] = [kb.sb(st, [128, 128], BF16, "wTb") for _ in R2]
            X["usb"] = [kb.sb(st, [128, 128], F32, "usb") for _ in R2]
            X["vnb"] = [kb.sb(st, [128, 128], BF16, "vnb") for _ in R2]
            X["osb"] = [kb.sb(st, [128, 128], F32, "gosb") for _ in R2]
            X["zsb"] = [kb.sb(st, [128, 128], F32, "gzsb") for _ in R2]
            X["col"] = [kb.sb(st, [128, 4], F32, "gcol") for _ in R2]
            X["junk"] = kb.sb(st, [128, 128], F32, "gjunk")
            bA, bB, bC, bD = banks[4 * sl:4 * sl + 4]
            X["banks"] = (bA, bB, bC, bD)
            X["r124"] = Buf(bA.t[:, 0:384], "r124", share=bA)
            X["r3"] = Buf(bA.t[:, 384:512], "r3", share=bA)
            X["kq"] = Buf(bB.t[:, 0:256], "kq", share=bB)
            X["n2"] = Buf(bB.t[:, 256:512], "n2", share=bB)
            X["pP"] = Buf(bC.t[:, 0:128], "pP", share=bC)
            X["pWt"] = Buf(bC.t[:, 128:256], "pWt", share=bC)
            X["pU"] = Buf(bC.t[:, 256:384], "pU", share=bC)
            X["pWS"] = Buf(bC.t[:, 384:512], "pWS", share=bC)
            X["pO"] = Buf(bD.t[:, 0:128], "pO", share=bD)
            X["pOT"] = Buf(bD.t[:, 128:256], "pOT", share=bD)
            X["pSn"] = Buf(bD.t[:, 256:384], "pSn", share=bD)
            slots.append(X)

        pn0 = banks[0]
        kb.op("dve", lambda e: e.tensor_copy(out=g6[:], in_=rate[:, :, 22:28]), reads=[rate], writes=[g6])
        g6f = g6[:].rearrange("p a b -> p (a b)")
        n6 = nblk * 6
        kb.op("pe", lambda e: e.matmul(pn0[:, 0:n6], lhsT=C["tri_f"][:], rhs=g6f, start=True, stop=True),
              reads=[C["tri_f"], g6], writes=[pn0])
        kb.op("dve", lambda e: e.tensor_copy(out=gc[:].rearrange("p a b -> p (a b)"), in_=pn0[:, 0:n6]), reads=[pn0], writes=[gc])
        kb.op("pe", lambda e: e.matmul(pn0[:, 0:n6], lhsT=C["ones_f"][:], rhs=g6f, start=True, stop=True),
              reads=[C["ones_f"], g6], writes=[pn0])
        kb.op("dve", lambda e: e.tensor_copy(out=tot[:].rearrange("p a b -> p (a b)"), in_=pn0[:, 0:n6]), reads=[pn0], writes=[tot])
        kb.op("dve", lambda e: e.tensor_tensor(out=etl[:], in0=tot[:], in1=gc[:], op=ALU.subtract), reads=[tot, gc], writes=[etl])
        kb.op("act", lambda e: e.activation(out=etl[:], in_=etl[:], func=AF.Exp), reads=[etl], writes=[etl])
        kb.op("act", lambda e: e.activation(out=els[:], in_=tot[:], func=AF.Exp), reads=[tot], writes=[els])
        kb.op("act", lambda e: e.activation(out=kbs[:], in_=gc[:], func=AF.Exp), reads=[gc], writes=[kbs])
        kb.op("dve", lambda e: e.tensor_tensor(out=kbs[:], in0=kbs[:], in1=betaT[:, :, 16:22], op=ALU.mult),
              reads=[kbs, betaT], writes=[kbs])
        kb.barrier()

        def setup(X, h):
            bA, bB, bE, bF = X["banks"]
            W["pn"] = [bA, bB]
            qT, kTb = X["qT"], X["kTb"]
            conv_silu(kb, W, proj_d, EC_GQ + h, PC["gconv"] + 4 * h, None, tmp, S, True)
            pnorm(kb, W, tmp, qT, S, 1.0, None, 128.0 ** -0.5)
            conv_silu(kb, W, proj_d, EC_GK + h, PC["gconv"] + 4 * (6 + h), None, tmp, S, True)
            pnorm(kb, W, tmp, kTb, S, 1.0, None, 1.0)
            conv_silu(kb, W, proj_d, EC_GV + h, PC["gconv"] + 4 * (12 + h), None, vTb, S, False)
            zrows = proj_d[(EC_GZ + h) * 128:(EC_GZ + h + 1) * 128, 0:S]
            kb.dma("sp", W["craw"][:, 3:3 + S], zrows, [W["craw"]], [], W["craw"])
            kb.op("act", lambda e: e.activation(out=W["craw"][:, 3:3 + S], in_=W["craw"][:, 3:3 + S], func=AF.Silu),
                  reads=[W["craw"]], writes=[W["craw"]])
            kb.dma("sp", zrows, W["craw"][:, 3:3 + S], [Buf(None, "zreg")], [W["craw"]], W["craw"])
            for b0 in range(0, nblk, 4):
                nb = min(4, nblk - b0)
                for j in range(nb):
                    sl_ = slice((b0 + j) * 128, (b0 + j + 1) * 128)
                    kb.op("pe", lambda e: e.matmul(bE[:, j * 128:(j + 1) * 128], lhsT=kTb[:, sl_], rhs=C["ident_bf"][:],
                                                   start=True, stop=True), reads=[kTb, C["ident_bf"]], writes=[bE], inc=False)
                    kb.op("pe", lambda e: e.matmul(bF[:, j * 128:(j + 1) * 128], lhsT=vTb[:, sl_], rhs=C["ident_bf"][:],
                                                   start=True, stop=True), reads=[vTb, C["ident_bf"]], writes=[bF], inc=(j == nb - 1))

                def bcs(t, cc):
                    return t[:, b0:b0 + nb, cc:cc + 1].broadcast_to([128, nb, 128])
                kview = bE[:, 0:nb * 128].rearrange("p (a b) -> p a b", a=nb)
                vview = bF[:, 0:nb * 128].rearrange("p (a b) -> p a b", a=nb)
                kb.op("dve", lambda e: e.tensor_tensor(out=X["Kbg"][:, b0:b0 + nb, :], in0=kview, in1=bcs(kbs, h), op=ALU.mult),
                      reads=[bE, kbs], writes=[X["Kbg"]])
                kb.op("dve", lambda e: e.tensor_tensor(out=X["Ktl"][:, b0:b0 + nb, :], in0=kview, in1=bcs(etl, h), op=ALU.mult),
                      reads=[bE, etl], writes=[X["Ktl"]])
                kb.op("dve", lambda e: e.tensor_tensor(out=X["Vb"][:, b0:b0 + nb, :], in0=vview, in1=bcs(betaT, 16 + h), op=ALU.mult),
                      reads=[bF, betaT], writes=[X["Vb"]])
            kb.op("dve", lambda e: e.memset(X["Sst"][:], 0.0), reads=[], writes=[X["Sst"]])
            kb.op("dve", lambda e: e.memset(X["Sbf"][:], 0.0), reads=[], writes=[X["Sbf"]])

        def chunks(X, h):
            qT, kTb, mixo = X["qT"], X["kTb"], X["mixo"]
            Sst, Sbf = X["Sst"], X["Sbf"]
            r124, r3, kq, n2, pP, pWt, pU, pWS, pO, pOT, pSn = (X[k] for k in
                                                                 ("r124", "r3", "kq", "n2", "pP", "pWt", "pU", "pWS", "pO", "pOT", "pSn"))

            def mm(out, l_, r_, s0, s1, rd, wr, inc=False):
                kb.op("pe", lambda e: e.matmul(out, lhsT=l_, rhs=r_, start=s0, stop=s1), reads=rd, writes=[wr], inc=inc)
            def Pgen(c):
                u = c % 2
                blk = slice(c * 128, (c + 1) * 128)
                gcol = rate[:, c, 22 + h:23 + h]
                lnb = rate[:, c, 16 + h:17 + h]
                at, dl, E124, E3 = X["at"][u], X["dl"][u], X["E124"][u], X["E3"][u]
                zsb = X["zsb"][u]
                zr0 = (EC_GZ + h) * 128
                kb.dma("sp", zsb[:], proj_d[zr0:zr0 + 128, blk], [zsb], [], zsb)
                kb.op("dve", lambda e: e.tensor_scalar(out=at[:], in0=C["tri_f"][:], scalar1=gcol, scalar2=None, op0=ALU.mult),
                      reads=[C["tri_f"], rate], writes=[at])
                kb.op("dve", lambda e: e.tensor_scalar(out=dl[:], in0=idf[:], scalar1=lnb, scalar2=None, op0=ALU.mult),
                      reads=[idf, rate], writes=[dl])
                yield
                mm(r124[:, 0:128], C["striT"][:], at[:], True, False, [C["striT"], at], r124)
                mm(r124[:, 0:128], idf[:], C["negm_ns"][:], False, True, [idf, C["negm_ns"]], r124)
                mm(r124[:, 128:256], C["striT"][:], at[:], True, False, [C["striT"], at], r124)
                mm(r124[:, 128:256], C["ones_f"][:], dl[:], False, False, [C["ones_f"], dl], r124)
                mm(r124[:, 128:256], idf[:], C["negm_s"][:], False, True, [idf, C["negm_s"]], r124)
                mm(r124[:, 256:384], C["ones_f"][:], at[:], True, True, [C["ones_f"], at], r124, True)
                mm(r3[:], at[:], C["striT"][:], True, False, [C["striT"], at], r3)
                mm(r3[:], idf[:], C["negm_sT"][:], False, True, [idf, C["negm_sT"]], r3, True)
                mm(kq[:, 0:128], kTb[:, blk], kTb[:, blk], True, True, [kTb], kq)
                mm(kq[:, 128:256], kTb[:, blk], qT[:, blk], True, True, [kTb, qT], kq, True)
                yield
                kb.op("act", lambda e: e.activation(out=E124[:], in_=r124[:], func=AF.Exp), reads=[r124], writes=[E124])
                kb.op("act", lambda e: e.activation(out=E3[:], in_=r3[:], func=AF.Exp, bias=lnb), reads=[r3, rate], writes=[E3])
                yield
                X0 = X["XX"][0]
                kb.op("dve", lambda e: e.scalar_tensor_tensor(out=X0[:, 0:128].bitcast(F32R), in0=kq[:, 0:128], scalar=-1.0,
                                                              in1=E124[:, 128:256], op0=ALU.mult, op1=ALU.mult),
                      reads=[kq, E124], writes=[X0])
                kb.op("dve", lambda e: e.scalar_tensor_tensor(out=X0[:, 128:256].bitcast(F32R), in0=kq[:, 0:128], scalar=-1.0,
                                                              in1=E3[:], op0=ALU.mult, op1=ALU.mult),
                      reads=[kq, E3], writes=[X0])
                kb.op("dve", lambda e: e.tensor_tensor(out=X["Pm"][0][:].bitcast(F32R), in0=idf[:], in1=X0[:, 0:128], op=ALU.add),
                      reads=[idf, X0], writes=[X["Pm"][0]])
                kb.op("dve", lambda e: e.tensor_tensor(out=X["qkT"][u][:], in0=kq[:, 128:256], in1=E124[:, 0:128], op=ALU.mult),
                      reads=[kq, E124], writes=[X["qkT"][u]])
                kb.op("dve", lambda e: e.tensor_tensor(out=X["qdT"][u][:], in0=qT[:, blk], in1=E124[:, 256:384], op=ALU.mult),
                      reads=[qT, E124], writes=[X["qdT"][u]])
                yield
                xi, pi = 0, 0
                for lev in range(1, 7):
                    Xc, Pc = X["XX"][xi], X["Pm"][pi]
                    Xn = X["XX"][(xi + 1) % 3]
                    if lev < 6:
                        mm(n2[:, 0:128], Xc[:, 128:256].bitcast(F32R), Xc[:, 0:128].bitcast(F32R), True, True, [Xc], n2)
                    mm(n2[:, 128:256], Xc[:, 0:128].bitcast(F32R), Xc[:, 128:256].bitcast(F32R), True, True, [Xc], n2, True)
                    yield
                    if lev < 6:
                        kb.op("act", lambda e: e.activation(out=Xn[:].bitcast(F32R), in_=n2[:], func=AF.Copy), reads=[n2], writes=[Xn])
                    else:
                        kb.op("act", lambda e: e.activation(out=Xn[:, 128:256].bitcast(F32R), in_=n2[:, 128:256], func=AF.Copy),
                              reads=[n2], writes=[Xn])
                    yield
                    mm(pP[:], Xn[:, 128:256].bitcast(F32R), Pc[:].bitcast(F32R), True, True, [Xn, Pc], pP, True)
                    yield
                    if lev < 6:
                        Pn = X["Pm"][(pi + 1) % 3]
                        kb.op("dve", lambda e: e.tensor_tensor(out=Pn[:].bitcast(F32R), in0=pP[:], in1=Pc[:], op=ALU.add),
                              reads=[pP, Pc], writes=[Pn])
                        pi = (pi + 1) % 3
                    else:
                        kb.op("dve", lambda e: e.tensor_tensor(out=X["TTb"][u][:], in0=pP[:], in1=Pc[:], op=ALU.add),
                              reads=[pP, Pc], writes=[X["TTb"][u]])
                    xi = (xi + 1) % 3
                    yield
                TTb, wTb, usb, vnb = X["TTb"][u], X["wTb"][u], X["usb"][u], X["vnb"][u]
                mm(pWt[:], X["Kbg"][:, c, :], TTb[:], True, True, [X["Kbg"], TTb], pWt, True)
                mm(pU[:], TTb[:], X["Vb"][:, c, :], True, True, [TTb, X["Vb"]], pU, True)
                yield
                kb.op("act", lambda e: e.activation(out=wTb[:], in_=pWt[:], func=AF.Copy), reads=[pWt], writes=[wTb])
                kb.op("act", lambda e: e.activation(out=usb[:], in_=pU[:], func=AF.Copy), reads=[pU], writes=[usb])
                yield
                yield

            def Dgen(c):
                u = c % 2
                blk = slice(c * 128, (c + 1) * 128)
                zsb = X["zsb"][u]
                TTb, wTb, usb, vnb = X["TTb"][u], X["wTb"][u], X["usb"][u], X["vnb"][u]
                mm(pWS[:], wTb[:], Sbf[:], True, True, [wTb, Sbf], pWS, True)
                yield
                kb.op("dve", lambda e: e.tensor_tensor(out=vnb[:], in0=usb[:], in1=pWS[:], op=ALU.subtract),
                      reads=[usb, pWS], writes=[vnb])
                yield
                mm(pO[:], X["qdT"][u][:], Sbf[:], True, False, [X["qdT"][u], Sbf], pO)
                mm(pO[:], X["qkT"][u][:], vnb[:], False, True, [X["qkT"][u], vnb], pO, True)
                mm(pSn[:], X["Ktl"][:, c, :], vnb[:], True, True, [X["Ktl"], vnb], pSn, True)
                yield
                kb.op("dve", lambda e: e.scalar_tensor_tensor(out=Sst[:], in0=Sst[:], scalar=els[:, c, h:h + 1], in1=pSn[:],
                                                              op0=ALU.mult, op1=ALU.add), reads=[Sst, els, pSn], writes=[Sst])
                kb.op("act", lambda e: e.activation(out=Sbf[:], in_=Sst[:], func=AF.Copy), reads=[Sst], writes=[Sbf])
                c_ = X["col"][u]
                osb = X["osb"][u]
                kb.op("act", lambda e: e.activation(out=X["junk"][:], in_=pO[:], func=AF.Square, accum_out=c_[:, 0:1]),
                      reads=[pO], writes=[X["junk"], c_])
                kb.op("act", lambda e: e.activation(out=c_[:, 1:2], in_=c_[:, 0:1], func=AF.Ln, scale=1.0 / 128,
                                                    bias=C["eps"][:]), reads=[c_, C["eps"]], writes=[c_])
                yield
                kb.op("act", lambda e: e.activation(out=c_[:, 2:3], in_=c_[:, 1:2], func=AF.Exp, scale=-0.5), reads=[c_], writes=[c_])
                kb.op("dve", lambda e: e.tensor_scalar(out=osb[:], in0=pO[:], scalar1=c_[:, 2:3], scalar2=None, op0=ALU.mult),
                      reads=[pO, c_], writes=[osb])
                yield
                kb.op("pe", lambda e: e.transpose(out=pOT[:], in_=osb[:], identity=idf[:]), reads=[osb, idf], writes=[pOT])
                yield
                kb.op("dve", lambda e: e.scalar_tensor_tensor(out=mixo[:, blk], in0=pOT[:], scalar=pc[:, PC["gnw"]:PC["gnw"] + 1],
                                                              in1=zsb[:], op0=ALU.mult, op1=ALU.mult),
                      reads=[pOT, pc, zsb], writes=[mixo])
                yield

            yield from Pgen(0)
            for c in range(nblk):
                active = [Dgen(c)]
                if c + 1 < nblk:
                    active.insert(0, Pgen(c + 1))
                while active:
                    for g_ in list(active):
                        try:
                            next(g_)
                            yield
                        except StopIteration:
                            active.remove(g_)
            r0 = 1280 + h * 128
            kb.dma("sp", mix_d[:, r0:r0 + 128, :].rearrange("n p t -> p n t"),
                   mixo[:].rearrange("p (n t) -> p n t", t=mix_th(S)), [Buf(None, "m")], [mixo], mixo)

        for hp in range(3):
            for sl in range(2):
                setup(slots[sl], 2 * hp + sl)
            kb.barrier()
            run_interleaved([chunks(slots[sl], 2 * hp + sl) for sl in range(2)])
            kb.barrier()
        kb.release_dsems()


HD = D // 2


def mix_th(S):
    return min(512, S)


def stage_outproj(kb, S, xres_d, mixg_d, mixg_bufs, wout_d, dst_d):
    TH = min(512, S)
    NDB = HD // 512
    nth = S // TH
    with ExitStack() as st:
        mT = [kb.sb(st, [128, NKC, TH], BF16, "mT") for _ in range(2)]
        wb = [kb.sb(st, [128, NKC, 512], BF16, "wob") for _ in range(2)]
        xt = [kb.sb(st, [128, 512], F32, "xt") for _ in range(3)]
        ot = [kb.sb(st, [128, 512], F32, "ot") for _ in range(3)]
        pp = [kb.ps(st, [128, 512], F32, "pp") for _ in range(3)]

        def load_w(db):
            w_ = wb[db % 2]
            for q4 in range(4):
                kb.dma("pool", w_[:, q4 * 8:(q4 + 1) * 8, :],
                       wout_d[q4 * 1024:(q4 + 1) * 1024, db * 512:(db + 1) * 512].rearrange("(a p) d -> p a d", p=128),
                       [w_], [], w_)

        seq = [(db, th) for db in range(NDB) for th in range(nth)]

        def load_m(k):
            th = seq[k][1]
            m_ = mT[k % 2]
            for q4 in range(4):
                kb.dma("sp", m_[:, q4 * 8:(q4 + 1) * 8, :],
                       mixg_d[th, q4 * 1024:(q4 + 1) * 1024, :].rearrange("(a p) t -> p a t", p=128),
                       [m_], [mixg_bufs[th]], m_)

        load_w(0)
        load_m(0)
        i = 0
        for k, (db, th) in enumerate(seq):
            if th == 0 and db + 1 < NDB:
                load_w(db + 1)
            if k + 1 < len(seq):
                load_m(k + 1)
            w_, m_ = wb[db % 2], mT[k % 2]
            for tb in range(TH // 128):
                t0 = th * TH + tb * 128
                x_, o_, p_ = xt[i % 3], ot[i % 3], pp[i % 3]
                i += 1
                kb.dma("act", x_[:], xres_d[t0:t0 + 128, db * 512:(db + 1) * 512], [x_], [], x_)
                for ec in range(NKC):
                    kb.op("pe", lambda e: e.matmul(p_[:], lhsT=m_[:, ec, tb * 128:(tb + 1) * 128], rhs=w_[:, ec, :],
                                                   start=(ec == 0), stop=(ec == NKC - 1)),
                          reads=[m_, w_], writes=[p_], inc=(ec == NKC - 1))
                kb.op("dve", lambda e: e.tensor_tensor(out=o_[:], in0=p_[:], in1=x_[:], op=ALU.add),
                      reads=[p_, x_], writes=[o_])
                kb.dma("act", dst_d[t0:t0 + 128, db * 512:(db + 1) * 512], o_[:], [Buf(None, "o")], [o_], o_)
        kb.barrier()
        kb.release_dsems()


PAIRS = [[0, 1], [2, 3], [4, 5], [6, 7]]
XCH = 256


def mix_chunk_rows(S):
    return max(1, min(2048, (1 << 21) // (2 * S)))


def build_fused(S, depth=2):
    nc = bass.Bass("TRN2", target_bir_lowering=False)
    kb = KB(nc)
    x_d = nc.dram_tensor("x", [S, D], F32, kind="ExternalInput").ap()
    xres_d = nc.dram_tensor("xres", [S, HD], F32, kind="ExternalInput").ap()
    win_d = nc.dram_tensor("win", [depth, NEC, 128, NKC * 128], F32, kind="ExternalInput").ap()
    nwb_d = nc.dram_tensor("nwb", [depth, 128, D], F32, kind="ExternalInput").ap()
    pc_d = nc.dram_tensor("pc", [depth, 128, NPC], F32, kind="ExternalInput").ap()
    pr_d = nc.dram_tensor("pr", [depth, 128, NPR], F32, kind="ExternalInput").ap()
    wout_d = nc.dram_tensor("wout", [depth, D, HD], F32, kind="ExternalInput").ap()
    out_d = nc.dram_tensor("out", [S, HD], F32, kind="ExternalOutput").ap()
    proj_d = nc.dram_tensor("proj", [NEC * 128, S], F32).ap()
    ca = const_arrays()
    cd = {k: (nc.dram_tensor("c_" + k, list(v.shape), F32, kind="ExternalInput").ap(), list(v.shape),
              CONST_DT.get(k, F32)) for k, v in ca.items()}
    with ExitStack() as st:
        C = load_consts(kb, st, cd)
        xg_d = None
        xh_d = xres_d
        for l in range(depth):
            TM = mix_th(S)
            mixh_d = nc.dram_tensor("mixh%d" % l, [S // TM, 2048, TM], MIXDT).ap()
            mixg_d = nc.dram_tensor("mixg%d" % l, [S // TM, 4096, TM], MIXDT).ap()
            xload = None
            if l > 0:
                xg = xg_d
                xgs = xgs_prev

                def xload(kb_, xs_, j, xg=xg, xgs=xgs):
                    r0 = (j * 128 // XCH) * 2 * XCH + (j * 128) % XCH
                    rb = [xgs[j * 128 // XCH]]
                    kb_.dma("sp", xs_[:, 0:HD], xg[r0:r0 + 128, :], [xs_], rb, xs_)
                    kb_.dma("sp", xs_[:, HD:D], xg[r0 + XCH:r0 + XCH + 128, :], [xs_], rb, xs_)
            stage_inproj(kb, S, x_d, win_d[l], nwb_d[l], proj_d, C, TT=min(512, S), xload=xload)
            with ExitStack() as st2:
                W = alloc_common(kb, st2, S, C, pc_d[l], pr_d[l])
                with ExitStack() as st3:
                    W["pn"] = [kb.ps(st3, [128, 512], F32, "pn") for _ in range(2)]
                    T = stage_small(kb, st2, W, S, proj_d)
                    stage_fox(kb, W, T, S, proj_d, mixh_d)
                    stage_ssd(kb, W, T, S, proj_d, mixh_d)
                stage_gdn(kb, W, T, S, proj_d, mixh_d)
                kb.barrier()
                kb.release_dsems()
            mg = [Buf(None, "mixg") for _ in range(S // TM)]
            for k in range(S // TM):
                kb.collective("AllGather", PAIRS, mixh_d[k], mixg_d[k], Buf(None, "mixh"), mg[k])
            if l + 1 < depth:
                nxh_d = nc.dram_tensor("xh%d" % (l + 1), [S, HD], F32).ap()
                nxg_d = nc.dram_tensor("xg%d" % (l + 1), [2 * S, HD], F32).ap()
                stage_outproj(kb, S, xh_d, mixg_d, mg, wout_d[l], nxh_d)
                xgs = [Buf(None, "xg") for _ in range(S // XCH)]
                for k in range(S // XCH):
                    kb.collective("AllGather", PAIRS, nxh_d[k * XCH:(k + 1) * XCH, :],
                                  nxg_d[k * 2 * XCH:(k + 1) * 2 * XCH, :], Buf(None, "xh"), xgs[k])
                xh_d, xg_d = nxh_d, nxg_d
                xgs_prev = xgs
            else:
                stage_outproj(kb, S, xh_d, mixg_d, mg, wout_d[l], out_d)
        kb.barrier()
    return nc, ca


def mix_rows(hh):
    r = np.arange
    return np.concatenate([hh * 512 + r(512), 1024 + hh * 768 + r(768), 2560 + hh * 768 + r(768)])


def kernel(**inputs):
    inp = {k: np.asarray(v) for k, v in inputs.items()}
    x = np.ascontiguousarray(inp["x"], dtype=np.float32)
    B, S, _ = x.shape
    depth = inp["w_in"].shape[0]
    nc, ca = build_fused(S, depth)
    cores = list(range(8))
    rowperm = np.concatenate([mix_rows(0), mix_rows(1)])
    per_hh = []
    for hh in range(2):
        win = np.stack([prep_win(inp["w_in"][l], hh) for l in range(depth)])
        pps = [prep_params(inp, l, hh) for l in range(depth)]
        pc = np.stack([p[0] for p in pps])
        pr = np.stack([p[1] for p in pps])
        wout = np.stack([np.ascontiguousarray(inp["w_out"][l][rowperm][:, hh * HD:(hh + 1) * HD]) for l in range(depth)])
        per_hh.append((win, pc, pr, wout.astype(np.float32)))
    nwb = np.stack([np.broadcast_to(inp["norm_w"][l].astype(np.float32), (128, D)) for l in range(depth)]).copy()
    maps = []
    for c in cores:
        b, hh = c // 2, c % 2
        win, pc, pr, wout = per_hh[hh]
        m = {"x": x[b], "xres": np.ascontiguousarray(x[b][:, hh * HD:(hh + 1) * HD]), "win": win, "nwb": nwb,
             "pc": pc, "pr": pr, "wout": wout}
        for k, v in ca.items():
            m["c_" + k] = v
        maps.append(m)
    res = run_bass_kernel_spmd(nc, maps, core_ids=cores)
    out = np.empty((B, S, D), np.float32)
    for c in cores:
        b, hh = c // 2, c % 2
        out[b, :, hh * HD:(hh + 1) * HD] = res.results[c]["out"]
    return out
```
